# Optimizing a Trainium2 kernel written in Bass

```python
import math
import jax, jax.numpy as jnp
from jax import lax
import numpy as np

D_MODEL = 2048
BATCH = 4
SEQ = 8192
DEPTH = 1

PLE_DIM = 256
MIX_WIDTH = D_MODEL
ATTN_WIDTH = MIX_WIDTH // 2
HEAD_DIM = 64
N_HEADS = ATTN_WIDTH // HEAD_DIM
SSM_WIDTH = MIX_WIDTH - ATTN_WIDTH
SSM_GROUP = 16
N_SSM_GROUPS = SSM_WIDTH // SSM_GROUP
SSM_STATE = 64
D_FF = ((8 * D_MODEL // 3 + 127) // 128) * 128
DILATIONS = ((128, 1), (512, 4), (2048, 16))
SWA_BLOCK = 128
NORM_EPS = 1e-6
DT_MIN = 1e-3
DT_MAX = 1e-1
MASK_VALUE = -1e30

kernel_name = 'hymba_dilated_s5_macaron'


def rms_norm(x, g):
    xf = x.astype(jnp.float32)
    y = xf * lax.rsqrt(jnp.mean(xf * xf, axis=-1, keepdims=True) + NORM_EPS)
    return (y * g.astype(jnp.float32)).astype(x.dtype)


def swiglu(x, w_gate, w_up, w_down):
    return (jax.nn.silu(x @ w_gate) * (x @ w_up)) @ w_down


def banded_attention_stats(q, k, v, span):
    N, H, L, E = q.shape
    Q = SWA_BLOCK
    nb = -(-L // Q)
    Lp = nb * Q
    qf, kf, vf = (t.astype(jnp.float32) for t in (q, k, v))
    qb = jnp.pad(qf, ((0, 0), (0, 0), (0, Lp - L), (0, 0))).reshape(N, H, nb, Q, E)
    kp = jnp.pad(kf, ((0, 0), (0, 0), (Q, Lp - L), (0, 0)))
    vp = jnp.pad(vf, ((0, 0), (0, 0), (Q, Lp - L), (0, 0)))
    kb = jnp.concatenate([kp[:, :, :Lp].reshape(N, H, nb, Q, E), kp[:, :, Q:].reshape(N, H, nb, Q, E)], axis=3)
    vb = jnp.concatenate([vp[:, :, :Lp].reshape(N, H, nb, Q, E), vp[:, :, Q:].reshape(N, H, nb, Q, E)], axis=3)
    s = jnp.einsum('nhbqe,nhbke->nhbqk', qb, kb) * (E ** -0.5)
    qi = jnp.arange(Q)[:, None]
    ki = jnp.arange(2 * Q)[None, :]
    dist = qi + Q - ki
    blk = jnp.arange(nb)[:, None, None]
    valid = (dist >= 0) & (dist <= span) & (blk * Q + ki - Q >= 0)
    s = jnp.where(valid, s, MASK_VALUE)
    m = jnp.max(s, axis=-1)
    pexp = jnp.exp(s - m[..., None])
    l = jnp.sum(pexp, axis=-1)
    o = jnp.einsum('nhbqk,nhbke->nhbqe', pexp, vb)
    o = o.reshape(N, H, Lp, E)[:, :, :L]
    m = m.reshape(N, H, Lp)[:, :, :L]
    l = l.reshape(N, H, Lp)[:, :, :L]
    return o, m, l


def dilated_attention(q, k, v):
    B, S, H, E = q.shape
    outs, maxes, dens = [], [], []
    for window, d in DILATIONS:
        L = S // d
        span = window // d

        def to_residue(t):
            return t.reshape(B, L, d, H, E).transpose(0, 2, 3, 1, 4).reshape(B * d, H, L, E)

        o, m, l = banded_attention_stats(to_residue(q), to_residue(k), to_residue(v), span)
        outs.append(o.reshape(B, d, H, L, E).transpose(0, 3, 1, 2, 4).reshape(B, S, H, E))
        maxes.append(m.reshape(B, d, H, L).transpose(0, 3, 1, 2).reshape(B, S, H))
        dens.append(l.reshape(B, d, H, L).transpose(0, 3, 1, 2).reshape(B, S, H))
    m_all = jnp.stack(maxes, axis=0)
    m_glob = jnp.max(m_all, axis=0)
    w = jnp.exp(m_all - m_glob[None])
    num = sum(w[i][..., None] * outs[i] for i in range(len(DILATIONS)))
    den = sum(w[i] * dens[i] for i in range(len(DILATIONS)))
    return (num / den[..., None]).astype(q.dtype)


def _ssm_combine(left, right):
    ar_l, ai_l, br_l, bi_l = left
    ar_r, ai_r, br_r, bi_r = right
    return (ar_r * ar_l - ai_r * ai_l,
            ar_r * ai_l + ai_r * ar_l,
            ar_r * br_l - ai_r * bi_l + br_r,
            ar_r * bi_l + ai_r * br_l + bi_r)


def s5_mixer(u, lam_re, lam_im, log_dt, b_re, b_im, c_re, c_im, d_skip, w_glu, b_glu):
    B, S, _ = u.shape
    G, P, C = N_SSM_GROUPS, SSM_STATE, SSM_GROUP
    uf = u.astype(jnp.float32).reshape(B, S, G, C)
    lr = lam_re.astype(jnp.float32)
    li = lam_im.astype(jnp.float32)
    dt = jnp.exp(log_dt.astype(jnp.float32))[:, None]
    mag = jnp.exp(lr * dt)
    ar = mag * jnp.cos(li * dt)
    ai = mag * jnp.sin(li * dt)
    nr, ni = ar - 1.0, ai
    den = lr * lr + li * li
    cr = (nr * lr + ni * li) / den
    ci = (ni * lr - nr * li) / den
    br, bi = b_re.astype(jnp.float32), b_im.astype(jnp.float32)
    bbr = cr[..., None] * br - ci[..., None] * bi
    bbi = cr[..., None] * bi + ci[..., None] * br
    xr = jnp.einsum('gpc,bsgc->bsgp', bbr, uf)
    xi = jnp.einsum('gpc,bsgc->bsgp', bbi, uf)
    a_r = jnp.broadcast_to(ar[None, None], (1, S, G, P))
    a_i = jnp.broadcast_to(ai[None, None], (1, S, G, P))
    _, _, hr, hi = lax.associative_scan(_ssm_combine, (a_r, a_i, xr, xi), axis=1)
    y = (jnp.einsum('gcp,bsgp->bsgc', c_re.astype(jnp.float32), hr)
         - jnp.einsum('gcp,bsgp->bsgc', c_im.astype(jnp.float32), hi)
         + d_skip.astype(jnp.float32).reshape(G, C) * uf)
    y = jax.nn.gelu(y.reshape(B, S, SSM_WIDTH)).astype(u.dtype)
    return y * jax.nn.sigmoid(y @ w_glu + b_glu)


def setup_inputs(seed: int = 0) -> dict:
    key = jax.random.key(seed)
    ks = iter(jax.random.split(key, 40))

    def nrm(shape, scale):
        return jax.random.normal(next(ks), shape, jnp.float32) * scale

    def gain(shape):
        return 1.0 + nrm(shape, 0.02)

    L_ = DEPTH
    G, P, C = N_SSM_GROUPS, SSM_STATE, SSM_GROUP
    return {
        'x': nrm((BATCH, SEQ, D_MODEL), 1.0),
        'p': nrm((DEPTH, BATCH, SEQ, PLE_DIM), 1.0),
        'ffn1_norm': gain((L_, D_MODEL)),
        'ffn1_w_gate': nrm((L_, D_MODEL, D_FF), D_MODEL ** -0.5),
        'ffn1_w_up': nrm((L_, D_MODEL, D_FF), D_MODEL ** -0.5),
        'ffn1_w_down': nrm((L_, D_FF, D_MODEL), D_FF ** -0.5),
        'mix_norm': gain((L_, D_MODEL)),
        'w_in': nrm((L_, D_MODEL, 3 * ATTN_WIDTH + SSM_WIDTH), D_MODEL ** -0.5),
        'attn_out_norm': gain((L_, ATTN_WIDTH)),
        'ssm_lambda_re': -0.5 + nrm((L_, G, P), 0.01),
        'ssm_lambda_im': math.pi * jnp.arange(P, dtype=jnp.float32)[None, None, :] + nrm((L_, G, P), 0.01),
        'ssm_log_dt': jax.random.uniform(next(ks), (L_, G), jnp.float32, math.log(DT_MIN), math.log(DT_MAX)),
        'ssm_b_re': nrm((L_, G, P, C), (2.0 * C) ** -0.5),
        'ssm_b_im': nrm((L_, G, P, C), (2.0 * C) ** -0.5),
        'ssm_c_re': nrm((L_, G, C, P), (2.0 * P) ** -0.5),
        'ssm_c_im': nrm((L_, G, C, P), (2.0 * P) ** -0.5),
        'ssm_d': nrm((L_, SSM_WIDTH), 1.0),
        'ssm_w_glu': nrm((L_, SSM_WIDTH, SSM_WIDTH), SSM_WIDTH ** -0.5),
        'ssm_b_glu': nrm((L_, SSM_WIDTH), 0.01),
        'ssm_out_norm': gain((L_, SSM_WIDTH)),
        'w_out': nrm((L_, MIX_WIDTH, D_MODEL), MIX_WIDTH ** -0.5),
        'ffn2_norm': gain((L_, D_MODEL)),
        'ffn2_w_gate': nrm((L_, D_MODEL, D_FF), D_MODEL ** -0.5),
        'ffn2_w_up': nrm((L_, D_MODEL, D_FF), D_MODEL ** -0.5),
        'ffn2_w_down': nrm((L_, D_FF, D_MODEL), D_FF ** -0.5),
        'ple_norm': gain((L_, D_MODEL)),
        'ple_w_gate': nrm((L_, D_MODEL, D_MODEL), D_MODEL ** -0.5),
        'ple_w_proj': nrm((L_, PLE_DIM, D_MODEL), PLE_DIM ** -0.5),
        'final_norm': gain((D_MODEL,)),
    }


def reference(x, p, ffn1_norm, ffn1_w_gate, ffn1_w_up, ffn1_w_down, mix_norm, w_in,
              attn_out_norm, ssm_lambda_re, ssm_lambda_im, ssm_log_dt, ssm_b_re, ssm_b_im,
              ssm_c_re, ssm_c_im, ssm_d, ssm_w_glu, ssm_b_glu, ssm_out_norm, w_out,
              ffn2_norm, ffn2_w_gate, ffn2_w_up, ffn2_w_down, ple_norm, ple_w_gate,
              ple_w_proj, final_norm):
    B, S, _ = x.shape
    h = x
    for i in range(DEPTH):
        h = h + 0.5 * swiglu(rms_norm(h, ffn1_norm[i]), ffn1_w_gate[i], ffn1_w_up[i], ffn1_w_down[i])
        u = rms_norm(h, mix_norm[i])
        z = u @ w_in[i]
        q = z[..., :ATTN_WIDTH].reshape(B, S, N_HEADS, HEAD_DIM)
        k = z[..., ATTN_WIDTH:2 * ATTN_WIDTH].reshape(B, S, N_HEADS, HEAD_DIM)
        v = z[..., 2 * ATTN_WIDTH:3 * ATTN_WIDTH].reshape(B, S, N_HEADS, HEAD_DIM)
        s_in = z[..., 3 * ATTN_WIDTH:]
        ya = dilated_attention(q, k, v).reshape(B, S, ATTN_WIDTH)
        yb = s5_mixer(s_in, ssm_lambda_re[i], ssm_lambda_im[i], ssm_log_dt[i], ssm_b_re[i], ssm_b_im[i],
                      ssm_c_re[i], ssm_c_im[i], ssm_d[i], ssm_w_glu[i], ssm_b_glu[i])
        y = jnp.concatenate([rms_norm(ya, attn_out_norm[i]), rms_norm(yb, ssm_out_norm[i])], axis=-1)
        h = h + y @ w_out[i]
        h = h + 0.5 * swiglu(rms_norm(h, ffn2_norm[i]), ffn2_w_gate[i], ffn2_w_up[i], ffn2_w_down[i])
        gate = jax.nn.sigmoid(rms_norm(h, ple_norm[i]) @ ple_w_gate[i])
        h = h + gate * (p[i] @ ple_w_proj[i])
    return rms_norm(h, final_norm)
```

```python
import numpy as np
import ml_dtypes
import concourse.bass as bass
import concourse.mybir as mybir
from concourse.bass_utils import run_bass_kernel_spmd

F32 = mybir.dt.float32
BF16 = mybir.dt.bfloat16
I32 = mybir.dt.int32
U8 = mybir.dt.uint8
AF = mybir.ActivationFunctionType
ALU = mybir.AluOpType

P = 128
TT = 512
D = 2048
DC = 16
DFF = 5504
FC = 43
HALO = 2048
EPS = 1e-6
TWO_PI = float(2.0 * np.pi)
PI_SAFE = 3.1415925

WDEFS = [
    ("f1g", "ffn1_w_gate", D, DFF), ("f1u", "ffn1_w_up", D, DFF), ("f1d", "ffn1_w_down", DFF, D),
    ("win", "w_in", D, 4096), ("glu", "ssm_w_glu", 1024, 1024), ("wout", "w_out", D, D),
    ("f2g", "ffn2_w_gate", D, DFF), ("f2u", "ffn2_w_up", D, DFF), ("f2d", "ffn2_w_down", DFF, D),
    ("pg", "ple_w_gate", D, D), ("pp", "ple_w_proj", 256, D),
]
IN_SHAPES = {
    "ffn1_norm": [D], "ffn1_w_gate": [D, DFF], "ffn1_w_up": [D, DFF], "ffn1_w_down": [DFF, D],
    "mix_norm": [D], "w_in": [D, 4096], "attn_out_norm": [1024],
    "ssm_lambda_re": [64, 64], "ssm_lambda_im": [64, 64], "ssm_log_dt": [64],
    "ssm_b_re": [64, 64, 16], "ssm_b_im": [64, 64, 16], "ssm_c_re": [64, 16, 64], "ssm_c_im": [64, 16, 64],
    "ssm_d": [1024], "ssm_w_glu": [1024, 1024], "ssm_b_glu": [1024], "ssm_out_norm": [1024],
    "w_out": [D, D], "ffn2_norm": [D], "ffn2_w_gate": [D, DFF], "ffn2_w_up": [D, DFF], "ffn2_w_down": [DFF, D],
    "ple_norm": [D], "ple_w_gate": [D, D], "ple_w_proj": [256, D], "final_norm": [D],
}


class Res:
    __slots__ = ("name", "w", "r", "dsem", "dval")

    def __init__(self, name):
        self.name = name
        self.w = None
        self.r = {}
        self.dsem = None
        self.dval = 0


class Ctx:
    def __init__(self, nc):
        self.nc = nc
        self.eng = {"pe": nc.tensor, "act": nc.scalar, "dve": nc.vector, "pool": nc.gpsimd, "sp": nc.sync}
        self.sem = {e: nc.alloc_semaphore("s_" + e) for e in self.eng}
        self.cnt = {e: 0 for e in self.eng}
        self.waited = {e: {} for e in self.eng}
        self.dry = False
        self.dres = []
        self.nsem = 5

    def _wait_ev(self, e, ev):
        if ev is None:
            return
        if ev[0] == "e":
            _, pe_, seq = ev
            if pe_ == e and e == "pe":
                return
            key = pe_
            if self.waited[e].get(key, 0) >= seq:
                return
            self.eng[e].wait_ge(self.sem[pe_], seq)
            self.waited[e][key] = seq
        else:
            res = ev[1]
            val = res.dval
            key = id(res)
            if self.waited[e].get(key, 0) >= val:
                return
            self.eng[e].wait_ge(res.dsem, val)
            self.waited[e][key] = val

    def _sync(self, e, reads, writes):
        for r in reads:
            self._wait_ev(e, r.w)
        for w in writes:
            self._wait_ev(e, w.w)
            for ev in w.r.values():
                self._wait_ev(e, ev)

    def _record(self, ev, key, reads, writes):
        for r in reads:
            r.r[key] = ev
        for w in writes:
            w.w = ev
            w.r = {}

    def op(self, e, fn, reads=(), writes=(), sig=True):
        if self.dry:
            return
        self._sync(e, reads, writes)
        ins = fn(self.eng[e])
        if sig:
            self.cnt[e] += 1
            ins.then_inc(self.sem[e], 1)
            ev = ("e", e, self.cnt[e])
        else:
            ev = ("e", e, self.cnt[e] + 1)
        self._record(ev, e, reads, writes)

    def pe(self, fn, reads=(), writes=(), sig=True):
        self.op("pe", fn, reads, writes, sig)

    def act(self, fn, reads=(), writes=()):
        self.op("act", fn, reads, writes)

    def dve(self, fn, reads=(), writes=()):
        self.op("dve", fn, reads, writes)

    def pool(self, fn, reads=(), writes=()):
        self.op("pool", fn, reads, writes)

    def dma(self, q, out, in_, reads=(), writes=(), sres=None, slow=False):
        if self.dry:
            return
        self._sync(q, reads, writes)
        if sres is None:
            sres = writes[0] if writes else reads[0]
        if sres.dsem is None:
            sres.dsem = self.nc.alloc_semaphore("d_" + sres.name)
            self.nsem += 1
            self.dres.append(sres)
        if slow:
            ins = self.eng[q].dma_start(out=out, in_=in_, allow_slow_non_contiguous=True)
        else:
            ins = self.eng[q].dma_start(out=out, in_=in_)
        sres.dval += 16
        ins.then_inc(sres.dsem, 16)
        ev = ("d", sres)
        self._record(ev, ("d", id(sres)), reads, writes)

    def barrier(self, engines=("pe", "act", "dve", "pool", "sp")):
        if self.dry:
            return
        for e in engines:
            for e2 in self.eng:
                if e2 != e and self.cnt[e2] > 0:
                    self._wait_ev(e, ("e", e2, self.cnt[e2]))
            for r in self.dres:
                self._wait_ev(e, ("d", r))


class WStream:
    NS = 6

    def __init__(self, ctx, slot_aps):
        self.ctx = ctx
        self.slots = slot_aps
        self.res = [Res("wslot%d" % i) for i in range(self.NS)]
        self.plan = []
        self.idx = 0
        self.issued = 0

    def request(self, src_ap, n_elem, src_res):
        if self.ctx.dry:
            self.plan.append((src_ap, n_elem, src_res))
            return None, None
        i = self.idx
        self.idx += 1
        self._issue_upto(i + self.NS - 1)
        s = i % self.NS
        return self.slots[s][:, 0:n_elem], self.res[s]

    def _issue_upto(self, j):
        j = min(j, len(self.plan) - 1)
        while self.issued <= j:
            src, n, sres = self.plan[self.issued]
            s = self.issued % self.NS
            self.ctx.dma("sp", self.slots[s][:, 0:n], src, reads=[sres], writes=[self.res[s]], sres=self.res[s])
            self.issued += 1


def build(NOWN, NPRE, debug=None):
    nc = bass.Bass("TRN2", target_bir_lowering=False)
    NTO = NOWN // TT
    NTP = NPRE // TT
    NT = NTO + NTP
    NKV = HALO + NOWN
    assert NPRE >= HALO and NOWN % 2048 == 0

    def din(name, shape, dt=F32):
        return nc.dram_tensor(name, list(shape), dt, kind="ExternalInput").ap()

    x_own = din("x_own", [NOWN, D])
    x_pre = din("x_pre", [NPRE, D])
    p_own = din("p_own", [NOWN, 256])
    hmask_in = din("hmask", [P, P])
    ident_in = din("ident", [P, P])
    tv_in = din("tv", [P, P])
    lmask_in = din("lmask", [P, P])
    umask_in = din("umask", [P, P])
    WIN = {k: din(k, s) for k, s in IN_SHAPES.items()}
    out = nc.dram_tensor("out", [NOWN, D], F32, kind="ExternalOutput").ap()

    WS = {}
    WSres = {}
    WKM = {}
    for name, src, K, M in WDEFS:
        Kc, Mc = K // P, M // P
        WS[name] = nc.dram_tensor("ws_" + name, [Mc, P, Kc * P], BF16).ap()
        WSres[name] = Res("ws_" + name)
        WKM[name] = (src, Kc, Mc)
    sk = "ExternalOutput" if debug else "Internal"
    H1 = nc.dram_tensor("h1", [NTO, P, DC * TT], F32, kind=sk).ap()
    QT = nc.dram_tensor("qt", [8, P, NOWN], BF16, kind=sk).ap()
    KT = nc.dram_tensor("kt", [8, P, NKV], BF16, kind=sk).ap()
    VS = nc.dram_tensor("vs", [NKV, 1024], BF16, kind=sk).ap()
    YB = nc.dram_tensor("yb", [NTO, P, 8 * TT], BF16, kind=sk).ap()
    YA = nc.dram_tensor("ya", [8, P, NOWN], BF16, kind=sk).ap()
    r_H1, r_QT, r_KT, r_VS, r_YB, r_YA = (Res(n) for n in ("H1", "QT", "KT", "VS", "YB", "YA"))

    TOTAL = 206 * 1024
    BIG = nc.alloc_sbuf_tensor("big", [P, TOTAL], U8)
    cur = [0]

    def carve(nbytes):
        o = cur[0]
        cur[0] += (nbytes + 63) // 64 * 64
        assert cur[0] <= TOTAL, cur[0]
        return o

    def view(off, shape, dt):
        sz = {F32: 4, BF16: 2, I32: 4}[dt]
        n = int(np.prod(shape))
        ap = BIG[:, off:off + n * sz].bitcast(dt)
        if len(shape) == 2:
            return ap.rearrange("p (a b) -> p a b", b=shape[1])
        if len(shape) == 3:
            return ap.rearrange("p (a b c) -> p a b c", b=shape[1], c=shape[2])
        return ap

    def alloc(shape, dt):
        sz = {F32: 4, BF16: 2, I32: 4}[dt]
        return view(carve(int(np.prod(shape)) * sz), shape, dt)

    o_hT = carve(DC * TT * 4)
    o_xnT = carve(DC * TT * 2)
    o_U1 = carve(FC * TT * 2)
    hT = view(o_hT, [DC, TT], F32)
    xnT = view(o_xnT, [DC, TT], BF16)
    hid = view(o_U1, [FC, TT], BF16)
    o_xblk = cur[0]
    xblk = [alloc([D], F32) for _ in range(2)]
    wslots = [alloc([2048], BF16) for _ in range(WStream.NS)]
    uT = alloc([8, TT], BF16)
    lst_f = [alloc([512], F32) for _ in range(2)]
    lst_b = [alloc([512], BF16) for _ in range(2)]
    sgt = [alloc([TT], BF16) for _ in range(2)]
    sqb = [alloc([TT], BF16) for _ in range(2)]
    rt = alloc([TT], F32)
    rstd = alloc([TT], F32)
    Tp = alloc([2, TT], F32)
    Td = alloc([2, TT], F32)
    sigb = alloc([TT], BF16)
    YV = alloc([TT], F32)
    CT = alloc([4], F32)
    dummy = alloc([16], F32)
    qst = [alloc([TT], BF16) for _ in range(2)]
    o_vtok = cur[0]
    vtok = alloc([4, 1024], BF16)
    ident = alloc([P], F32)
    identb = alloc([P], BF16)
    onesb = alloc([P], BF16)
    gains = {k: alloc([n], F32) for k, n in (("ffn1_norm", 16), ("mix_norm", 16), ("attn_out_norm", 8),
                                             ("ssm_out_norm", 8), ("ffn2_norm", 16), ("ple_norm", 16),
                                             ("final_norm", 16), ("ssm_d", 8), ("ssm_b_glu", 8))}
    BT = alloc([8, 2, P], BF16)
    CM = alloc([2, 32, 32], BF16)
    COST = alloc([32, P], BF16)
    SINT = alloc([32, P], BF16)
    MAG = alloc([32], F32)
    R128c = alloc([32], F32)
    R128s = alloc([32], F32)
    CARRY = alloc([2, 32], F32)
    o_ssmtmp = o_U1
    o_stage = cur[0]

    PSF = nc.alloc_psum_tensor("psf", [P, 8, TT], F32)
    PSB = PSF[:, 7, :].bitcast(BF16)
    r_ps = [Res("ps%d" % i) for i in range(8)]
    r_psb = [r_ps[7], r_ps[7]]
    ring = [0]

    def nbank():
        b = ring[0]
        ring[0] = (ring[0] + 1) % 3
        return b

    cx = Ctx(nc)
    ws = WStream(cx, wslots)

    r_hT = [Res("hT%d" % c) for c in range(DC)]
    r_xnT = Res("xnT")
    r_hid = Res("hid")
    r_xblk = [Res("xblk0"), Res("xblk1")]
    r_uT = Res("uT")
    r_sgt = [Res("sgt0"), Res("sgt1")]
    r_sqb = [Res("sq0"), Res("sq1")]
    r_rt, r_rstd = Res("rt"), Res("rstd")
    r_qst = [Res("qst0"), Res("qst1")]
    r_vtok = Res("vtok")
    qi_ = [0]
    r_const = Res("const")
    r_tab = Res("ssmtab")
    r_carry = Res("carry")

    bg = []

    def pump(n=1):
        for it in list(bg):
            for _ in range(n):
                try:
                    next(it[0])
                except StopIteration:
                    bg.remove(it)
                    break

    def drain(all_=False):
        while any(all_ or it[1] for it in bg):
            for it in list(bg):
                if all_ or it[1]:
                    try:
                        next(it[0])
                    except StopIteration:
                        bg.remove(it)

    def wpiece(name, m, k0, kn):
        return ws.request(WS[name][m][:, k0 * P:(k0 + kn) * P], kn * P, WSres[name])

    def linear(name, Kc, ms, rhs_fn, rhs_res, evac):
        for m in ms:
            b = nbank()
            for k0 in range(0, Kc, 16):
                kn = min(16, Kc - k0)
                sl, sres = wpiece(name, m, k0, kn)
                for i in range(kn):
                    kc = k0 + i
                    cx.pe(lambda e, sl=sl, i=i, kc=kc, b=b: e.matmul(
                        PSF[:, b, :], sl[:, i * P:(i + 1) * P], rhs_fn(kc), start=(kc == 0), stop=(kc == Kc - 1)),
                        [sres] + rhs_res, [r_ps[b]], sig=(i == kn - 1))
            evac(m, b)

    def norm(src_fn, src_res_fn, nch, gain, Dn, dst_fn, dst_res):
        b = nbank()
        for c in range(nch):
            q = c % 2
            cx.act(lambda e, c=c, q=q: e.activation(out=sqb[q][:], in_=src_fn(c), func=AF.Square),
                   [src_res_fn(c)], [r_sqb[q]])
            cx.pe(lambda e, c=c, q=q: e.matmul(PSF[:, b, :], onesb[:], sqb[q][:], start=(c == 0), stop=(c == nch - 1)),
                  [r_sqb[q], r_const], [r_ps[b]])
        cx.act(lambda e: e.activation(out=rt[:], in_=PSF[:, b, :], func=AF.Sqrt, scale=1.0 / Dn, bias=EPS),
               [r_ps[b]], [r_rt])
        cx.dve(lambda e: e.reciprocal(out=rstd[:], in_=rt[:]), [r_rt], [r_rstd])
        for c in range(nch):
            cx.dve(lambda e, c=c: e.scalar_tensor_tensor(out=dst_fn(c), in0=src_fn(c), scalar=gain[:, c:c + 1],
                                                         in1=rstd[:], op0=ALU.mult, op1=ALU.mult),
                   [src_res_fn(c), r_rstd, r_const], [dst_res])

    def ffn(pfx):
        for j in range(FC):
            bg = nbank()
            sl, sres = wpiece(pfx + "g", j, 0, 16)
            for kc in range(16):
                cx.pe(lambda e, sl=sl, kc=kc, bg=bg: e.matmul(PSF[:, bg, :], sl[:, kc * P:(kc + 1) * P], xnT[:, kc, :],
                                                             start=(kc == 0), stop=(kc == 15)),
                      [sres, r_xnT], [r_ps[bg]], sig=(kc == 15))
            pump(1)
            bu = nbank()
            sl, sres = wpiece(pfx + "u", j, 0, 16)
            for kc in range(16):
                cx.pe(lambda e, sl=sl, kc=kc, bu=bu: e.matmul(PSF[:, bu, :], sl[:, kc * P:(kc + 1) * P], xnT[:, kc, :],
                                                             start=(kc == 0), stop=(kc == 15)),
                      [sres, r_xnT], [r_ps[bu]], sig=(kc == 15))
            pump(1)
            q = j % 2
            cx.act(lambda e, bg=bg, q=q: e.activation(out=sgt[q][:], in_=PSF[:, bg, :], func=AF.Silu),
                   [r_ps[bg]], [r_sgt[q]])
            cx.act(lambda e, bu=bu, q=q: e.activation(out=sqb[q][:], in_=PSF[:, bu, :], func=AF.Identity),
                   [r_ps[bu]], [r_sqb[q]])
            cx.pool(lambda e, q=q, j=j: e.tensor_tensor(out=hid[:, j, :], in0=sqb[q][:], in1=sgt[q][:], op=ALU.mult),
                    [r_sqb[q], r_sgt[q]], [r_hid])
            pump(1)

        def ev(m, b):
            cx.dve(lambda e: e.scalar_tensor_tensor(out=hT[:, m, :], in0=PSF[:, b, :], scalar=0.5, in1=hT[:, m, :],
                                                    op0=ALU.mult, op1=ALU.add),
                   [r_ps[b], r_hT[m]], [r_hT[m]])
            pump(2)
        linear(pfx + "d", FC, range(DC), lambda kc: hid[:, kc, :], [r_hid], ev)

    def load_x_tile(xsrc, t0):
        for blk in range(4):
            q = blk % 2
            cx.dma("sp", xblk[q][:], xsrc[t0 + blk * P: t0 + (blk + 1) * P, :], writes=[r_xblk[q]])
            for g4 in range(4):
                b = nbank()
                for i in range(4):
                    c = g4 * 4 + i
                    cx.pe(lambda e, q=q, c=c, b=b, i=i: e.transpose(PSF[:, b, i * P:(i + 1) * P],
                                                                    xblk[q][:, c * P:(c + 1) * P], ident[:]),
                          [r_xblk[q], r_const], [r_ps[b]], sig=(i == 3))
                o = hT[:, g4 * 4:(g4 + 1) * 4, blk * P:(blk + 1) * P]
                i_ = PSF[:, b, :].rearrange("p (a b) -> p a b", b=P)
                rs = [r_hT[g4 * 4 + i] for i in range(4)]
                cx.act(lambda e, o=o, i_=i_: e.activation(out=o, in_=i_, func=AF.Identity), [r_ps[b]], rs)


    def p0_gen(names, knmax, stage_f, stage_b, engs, tag):
        nst = len(stage_f)
        r_sf = [Res("sf%s%d" % (tag, i)) for i in range(nst)]
        r_sb = [Res("sb%s%d" % (tag, i)) for i in range(nst)]
        pi = 0
        for name in names:
            src, Kc, Mc = WKM[name]
            for m in range(Mc):
                for k0 in range(0, Kc, knmax):
                    kn = min(knmax, Kc - k0)
                    s_ = pi % nst
                    srcap = WIN[src][k0 * P:(k0 + kn) * P, m * P:(m + 1) * P].rearrange("(kc p) j -> p kc j", p=P)
                    sf = stage_f[s_][:, 0:kn * P].rearrange("p (kc j) -> p kc j", j=P)
                    cx.dma("sp", sf, srcap, writes=[r_sf[s_]])
                    ce = engs[pi % len(engs)]
                    a_in = stage_f[s_][:, 0:kn * P]
                    a_out = stage_b[s_][:, 0:kn * P]
                    if ce == "act":
                        cx.act(lambda e: e.activation(out=a_out, in_=a_in, func=AF.Identity), [r_sf[s_]], [r_sb[s_]])
                    else:
                        cx.op(ce, lambda e: e.tensor_copy(out=a_out, in_=a_in), [r_sf[s_]], [r_sb[s_]])
                    cx.dma("pool", WS[name][m][:, k0 * P:(k0 + kn) * P], a_out, reads=[r_sb[s_]],
                           writes=[WSres[name]], sres=r_sb[s_])
                    pi += 1
                    yield

    def emit_all():
        cur[0] = o_stage
        ring[0] = 0
        cx.dma("sp", ident[:], ident_in, writes=[r_const])
        for k, g in gains.items():
            n = g.shape[1]
            cx.dma("sp", g[:], WIN[k].rearrange("(c p) -> p c", p=P), writes=[r_const], slow=True)
        cx.act(lambda e: e.activation(out=identb[:], in_=ident[:], func=AF.Identity), [r_const], [r_const])
        cx.dve(lambda e: e.memset(onesb[:], 1.0), [], [r_const])

        nst = 7
        stage_f = [view(i * 12288, [2048], F32) for i in range(nst)]
        stage_b = [view(i * 12288 + 8192, [2048], BF16) for i in range(nst)]
        list(p0_gen(("f1g", "f1u", "f1d", "win", "glu"), 16, stage_f, stage_b, ["dve", "act"], "e"))
        cx.barrier()

        ssm_setup()
        cx.barrier()

        bg.append([p0_gen(("wout", "f2g", "f2u", "f2d", "pg", "pp"), 4, lst_f, lst_b, ["act"], "l"), False])
        for ti in range(NT):
            pre = ti < NTP
            xsrc = x_pre if pre else x_own
            t0 = (ti if pre else ti - NTP) * TT
            io = ti - NTP
            load_x_tile(xsrc, t0)
            norm(lambda c: hT[:, c, :], lambda c: r_hT[c], DC, gains["ffn1_norm"], D, lambda c: xnT[:, c, :], r_xnT)
            ffn("f1")
            if not pre:
                cx.dma("pool", H1[io], hT.rearrange("p a b -> p (a b)"), reads=r_hT, writes=[r_H1], sres=r_hT[0])
            norm(lambda c: hT[:, c, :], lambda c: r_hT[c], DC, gains["mix_norm"], D, lambda c: xnT[:, c, :], r_xnT)
            kvpos = None
            if not pre:
                kvpos = HALO + t0
            elif t0 >= NPRE - HALO:
                kvpos = t0 - (NPRE - HALO)
            ms = []
            if not pre:
                ms += list(range(0, 8))
            if kvpos is not None:
                ms += list(range(8, 24))
            ms += list(range(24, 32))
            def ev_win(m, b):
                if m < 16:
                    q = qi_[0] % 2
                    qi_[0] += 1
                    cx.act(lambda e: e.activation(out=qst[q][:], in_=PSF[:, b, :], func=AF.Identity), [r_ps[b]], [r_qst[q]])
                    if m < 8:
                        cx.dma("pool", QT[m][:, t0:t0 + TT], qst[q][:], reads=[r_qst[q]], writes=[r_QT], sres=r_qst[q])
                    else:
                        cx.dma("pool", KT[m - 8][:, kvpos:kvpos + TT], qst[q][:], reads=[r_qst[q]], writes=[r_KT],
                               sres=r_qst[q])
                elif m < 24:
                    c = m - 16
                    q = qi_[0] % 2
                    qi_[0] += 1
                    cx.act(lambda e: e.activation(out=qst[q][:], in_=PSF[:, b, :], func=AF.Identity), [r_ps[b]], [r_qst[q]])
                    hb = c % 2
                    for blk in range(4):
                        cx.pe(lambda e, blk=blk: e.transpose(PSB[:, hb * 512 + blk * P: hb * 512 + (blk + 1) * P],
                                                             qst[q][:, blk * P:(blk + 1) * P], identb[:]),
                              [r_qst[q], r_const], [r_psb[hb]], sig=(blk == 3))
                    cx.dve(lambda e: e.tensor_copy(out=vtok[:, :, c * P:(c + 1) * P],
                                                   in_=PSB[:, hb * 512:(hb + 1) * 512].rearrange("p (a b) -> p a b", b=P)),
                           [r_psb[hb]], [r_vtok])
                    if c == 7:
                        cx.dma("pool", VS[kvpos:kvpos + TT, :].rearrange("(a p) f -> p a f", p=P), vtok[:],
                               reads=[r_vtok], writes=[r_VS], sres=r_vtok)
                else:
                    c = m - 24
                    cx.act(lambda e: e.activation(out=uT[:, c, :], in_=PSF[:, b, :], func=AF.Identity), [r_ps[b]], [r_uT])
            drain()
            linear("win", 16, ms, lambda kc: xnT[:, kc, :], [r_xnT], ev_win)
            bg.append([ssm_tile(pre, io), True])
        drain(True)
        cx.barrier()
        if debug and debug_stage[0] == "A":
            return
        attention()
        cx.barrier()
        phase_c()

    def ssm_setup():
        o0 = cur[0]
        cur[0] = 0
        r = r_tab
        LR0 = alloc([P], F32)
        LI0 = alloc([P], F32)
        LD0 = alloc([P], F32)
        ldt = alloc([2], F32)
        LR = alloc([32], F32)
        LI = alloc([32], F32)
        DT = alloc([32], F32)
        TH = alloc([32], F32)
        LDm = alloc([32], F32)
        T1 = alloc([32], F32)
        T2 = alloc([32], F32)
        T3 = alloc([32], F32)
        TI = alloc([32], I32)
        SN = alloc([32], F32)
        CS = alloc([32], F32)
        AR = alloc([32], F32)
        AI = alloc([32], F32)
        CR = alloc([32], F32)
        CI = alloc([32], F32)
        tv = alloc([P], F32)
        BR = alloc([32, 16], F32)
        BI = alloc([32, 16], F32)
        BBR = alloc([32, 16], F32)
        BBI = alloc([32, 16], F32)
        TB = alloc([32, 16], F32)
        BM = alloc([2, 32, 32], F32)
        C0 = [alloc([8, P], F32) for _ in range(2)]
        assert cur[0] <= o_xnT, cur[0]
        ANG = view(o_ssmtmp, [32, P], F32)
        ANG2 = view(o_ssmtmp + 16384, [32, P], F32)
        ANGI = view(o_xnT, [32, P], I32)

        def d(fn, reads=None, writes=None):
            cx.dve(fn, [r], [r])

        lam_r = WIN["ssm_lambda_re"].rearrange("(pr g2) p -> pr (g2 p)", g2=2)
        lam_i = WIN["ssm_lambda_im"].rearrange("(pr g2) p -> pr (g2 p)", g2=2)
        cx.dma("sp", LR0[0:32, :], lam_r, writes=[r])
        cx.dma("sp", LI0[0:32, :], lam_i, writes=[r])
        cx.dma("sp", ldt[0:32, :], WIN["ssm_log_dt"].rearrange("(pr g2) -> pr g2", g2=2), writes=[r])
        cx.dma("sp", tv[:], tv_in, writes=[r])
        cx.dma("sp", BR[:], WIN["ssm_b_re"].rearrange("(pr g2) p c -> (g2 p) pr c", g2=2), writes=[r])
        cx.dma("sp", BI[:], WIN["ssm_b_im"].rearrange("(pr g2) p c -> (g2 p) pr c", g2=2), writes=[r])
        for ri, nm in enumerate(("ssm_c_re", "ssm_c_im")):
            src = WIN[nm].rearrange("(cc pr4 g2) c p -> pr4 c cc g2 p", pr4=4, g2=2)
            for pr4 in range(4):
                for g2 in range(2):
                    cx.dma("sp", C0[ri][pr4 * 16:(pr4 + 1) * 16, :, g2 * 64:(g2 + 1) * 64],
                           src[pr4][:, :, g2, :], writes=[r])
        d(lambda e: e.tensor_copy(out=LD0[0:32, :].rearrange("p (a b) -> p a b", b=64),
                                  in_=ldt[0:32, :].unsqueeze(2).to_broadcast([32, 2, 64])))
        b = nbank()
        for i, (src, dst) in enumerate(((LR0, LR), (LI0, LI), (LD0, DT))):
            cx.pe(lambda e, src=src, i=i: e.transpose(PSF[:, b, i * 32:(i + 1) * 32], src[0:32, :], ident[0:32, 0:32]),
                  [r, r_const], [r_ps[b]])
        for i, dst in enumerate((LR, LI, DT)):
            cx.act(lambda e, i=i, dst=dst: e.activation(out=dst[:], in_=PSF[:, b, i * 32:(i + 1) * 32], func=AF.Identity),
                   [r_ps[b]], [r])
        cx.act(lambda e: e.activation(out=DT[:], in_=DT[:], func=AF.Exp), [r], [r])
        d(lambda e: e.tensor_tensor(out=TH[:], in0=LI[:], in1=DT[:], op=ALU.mult))
        d(lambda e: e.tensor_tensor(out=LDm[:], in0=LR[:], in1=DT[:], op=ALU.mult))
        cx.act(lambda e: e.activation(out=MAG[:], in_=LDm[:], func=AF.Exp), [r], [r])

        def sincos(ang, n, sin_out, cos_out, tmpa, tmpi):
            for shift, o in ((0.0, sin_out), (float(np.pi / 2), cos_out)):
                d(lambda e: e.tensor_scalar(out=tmpa, in0=ang, scalar1=shift, scalar2=1.0 / TWO_PI,
                                            op0=ALU.add, op1=ALU.mult))
                d(lambda e: e.tensor_copy(out=tmpi, in_=tmpa))
                d(lambda e: e.tensor_copy(out=tmpa, in_=tmpi))
                d(lambda e: e.scalar_tensor_tensor(out=tmpa, in0=tmpa, scalar=-TWO_PI, in1=ang, op0=ALU.mult, op1=ALU.add))
                d(lambda e: e.tensor_scalar(out=tmpa, in0=tmpa, scalar1=shift, scalar2=-PI_SAFE, op0=ALU.add, op1=ALU.max))
                d(lambda e: e.tensor_scalar(out=tmpa, in0=tmpa, scalar1=PI_SAFE, scalar2=None, op0=ALU.min))
                cx.act(lambda e, o=o: e.activation(out=o, in_=tmpa, func=AF.Sin), [r], [r])

        sincos(TH[:], 32, SN[:], CS[:], T1[:], TI[:])
        d(lambda e: e.tensor_tensor(out=AR[:], in0=MAG[:], in1=CS[:], op=ALU.mult))
        d(lambda e: e.tensor_tensor(out=AI[:], in0=MAG[:], in1=SN[:], op=ALU.mult))
        d(lambda e: e.tensor_scalar(out=T1[:], in0=AR[:], scalar1=-1.0, scalar2=None, op0=ALU.add))
        d(lambda e: e.tensor_tensor(out=T2[:], in0=LR[:], in1=LR[:], op=ALU.mult))
        d(lambda e: e.tensor_tensor(out=T3[:], in0=LI[:], in1=LI[:], op=ALU.mult))
        d(lambda e: e.tensor_tensor(out=T2[:], in0=T2[:], in1=T3[:], op=ALU.add))
        d(lambda e: e.reciprocal(out=T2[:], in_=T2[:]))
        d(lambda e: e.tensor_tensor(out=CR[:], in0=T1[:], in1=LR[:], op=ALU.mult))
        d(lambda e: e.tensor_tensor(out=T3[:], in0=AI[:], in1=LI[:], op=ALU.mult))
        d(lambda e: e.tensor_tensor(out=CR[:], in0=CR[:], in1=T3[:], op=ALU.add))
        d(lambda e: e.tensor_tensor(out=CR[:], in0=CR[:], in1=T2[:], op=ALU.mult))
        d(lambda e: e.tensor_tensor(out=CI[:], in0=AI[:], in1=LR[:], op=ALU.mult))
        d(lambda e: e.tensor_tensor(out=T3[:], in0=T1[:], in1=LI[:], op=ALU.mult))
        d(lambda e: e.tensor_tensor(out=CI[:], in0=CI[:], in1=T3[:], op=ALU.subtract))
        d(lambda e: e.tensor_tensor(out=CI[:], in0=CI[:], in1=T2[:], op=ALU.mult))
        crb = CR[:].unsqueeze(2).to_broadcast([P, 32, 16])
        cib = CI[:].unsqueeze(2).to_broadcast([P, 32, 16])
        d(lambda e: e.tensor_tensor(out=BBR[:], in0=BR[:], in1=crb, op=ALU.mult))
        d(lambda e: e.tensor_tensor(out=TB[:], in0=BI[:], in1=cib, op=ALU.mult))
        d(lambda e: e.tensor_tensor(out=BBR[:], in0=BBR[:], in1=TB[:], op=ALU.subtract))
        d(lambda e: e.tensor_tensor(out=BBI[:], in0=BI[:], in1=crb, op=ALU.mult))
        d(lambda e: e.tensor_tensor(out=TB[:], in0=BR[:], in1=cib, op=ALU.mult))
        d(lambda e: e.tensor_tensor(out=BBI[:], in0=BBI[:], in1=TB[:], op=ALU.add))
        d(lambda e: e.memset(BM[:], 0.0))
        for ri, src in enumerate((BBR, BBI)):
            d(lambda e, ri=ri, src=src: e.tensor_copy(out=BM[0:64, ri, :, 0:16], in_=src[0:64, :, :]))
            d(lambda e, ri=ri, src=src: e.tensor_copy(out=BM[64:128, ri, :, 16:32], in_=src[64:128, :, :]))
        for c in range(8):
            b = nbank()
            for ri in range(2):
                cx.pe(lambda e, c=c, ri=ri: e.transpose(PSF[:, b, ri * P:(ri + 1) * P],
                                                       BM[:, ri, 4 * c:4 * c + 4, :].rearrange("p a b -> p (a b)"), ident[:]),
                      [r, r_const], [r_ps[b]])
            cx.act(lambda e, c=c: e.activation(out=BT[:, c, :, :], in_=PSF[:, b, 0:2 * P].rearrange("p (a b) -> p a b", b=P),
                                               func=AF.Identity), [r_ps[b]], [r])
        d(lambda e: e.memset(CM[:], 0.0))
        for ri in range(2):
            for cc in range(8):
                b = nbank()
                cx.pe(lambda e, ri=ri, cc=cc: e.transpose(PSF[:, b, 0:64], C0[ri][0:64, cc, :], ident[0:64, 0:64]),
                      [r, r_const], [r_ps[b]])
                sc = 1.0 if ri == 0 else -1.0
                cx.act(lambda e, ri=ri, cc=cc, sc=sc: e.activation(
                    out=CM[0:64, ri, 4 * cc:4 * cc + 4, 0:16], in_=PSF[0:64, b, 0:64].rearrange("p (a b) -> p a b", b=16),
                    func=AF.Identity, scale=sc), [r_ps[b]], [r])
                cx.act(lambda e, ri=ri, cc=cc, sc=sc: e.activation(
                    out=CM[64:128, ri, 4 * cc:4 * cc + 4, 16:32], in_=PSF[64:128, b, 0:64].rearrange("p (a b) -> p a b", b=16),
                    func=AF.Identity, scale=sc), [r_ps[b]], [r])
        d(lambda e: e.tensor_tensor(out=ANG[:], in0=TH[:].unsqueeze(2).to_broadcast([P, 32, P]),
                                    in1=tv[:].unsqueeze(1).to_broadcast([P, 32, P]), op=ALU.mult))
        sincos(ANG[:], 4096, SINT[:], COST[:], ANG2[:], ANGI[:])
        d(lambda e: e.tensor_scalar(out=T3[:], in0=TH[:], scalar1=128.0, scalar2=None, op0=ALU.mult))
        sincos(T3[:], 32, R128s[:], R128c[:], T1[:], TI[:])
        cx.dve(lambda e: e.memset(CARRY[:], 0.0), [r], [r_carry])
        cur[0] = o0

    def ssm_tile(pre, io):
        ob = o_xblk
        XT = [view(ob, [2, TT], F32), Tp]
        HR = [view(ob + 4096 + i * 4096, [2, TT], F32) for i in range(2)]
        HB = [view(ob + 12288 + i * 2048, [2, TT], BF16) for i in range(2)]
        ybT = vtok.rearrange("p a b -> p (a b)").rearrange("p (a b) -> p a b", b=TT)
        r_XT = [Res("XT0"), Res("XT1")]
        r_Td, r_CT = Res("Td"), [Res("CT0"), Res("CT1")]
        r_HR = [Res("HR0"), Res("HR1")]
        r_HB = [Res("HB0"), Res("HB1")]
        r_YV, r_sg = Res("YV"), Res("sigb")
        allr = [r_XT[0], r_HR[0], r_HR[1], r_HB[0], r_HB[1]]
        cx.dve(lambda e: e.memset(dummy[:, 0:1], 0.0), r_xblk, allr)
        XBK = ((3, 4), (5, 6))
        YBK = 7
        t1 = Td[:, 0, :].rearrange("p (a b) -> p a b", b=P)
        t2 = Td[:, 1, :].rearrange("p (a b) -> p a b", b=P)

        def emit_x(pp_):
            for k, pr in enumerate((2 * pp_, 2 * pp_ + 1)):
                c, j = pr // 4, pr % 4
                XB = XBK[k]
                kw = {"tile_position": (96, 0)} if j == 3 else {}
                for ri in range(2):
                    cx.pe(lambda e: e.matmul(PSF[:, XB[ri], :], BT[32 * j:32 * j + 32, c, ri, :],
                                             uT[32 * j:32 * j + 32, c, :], start=True, stop=True, **kw),
                          [r_tab, r_uT], [r_ps[XB[ri]]])

        def emit_y(pp_):
            for k, pr in enumerate((2 * pp_, 2 * pp_ + 1)):
                c, j = pr // 4, pr % 4
                kw2 = {"tile_position": (0, 96)} if j == 3 else {}
                for ri in range(2):
                    cx.pe(lambda e: e.matmul(PSF[32 * j:32 * j + 32, YBK, :], CM[:, ri, pr, :], HB[k][:, ri, :],
                                             start=(ri == 0), stop=(ri == 1), **kw2), [r_tab, r_HB[k]], [r_ps[YBK]])
                if j == 3:
                    cx.dve(lambda e: e.scalar_tensor_tensor(out=YV[:], in0=uT[:, c, :], scalar=gains["ssm_d"][:, c:c + 1],
                                                            in1=PSF[:, YBK, :], op0=ALU.mult, op1=ALU.add),
                           [r_uT, r_ps[YBK], r_const], [r_YV])
                    cx.act(lambda e: e.activation(out=ybT[:, c, :], in_=YV[:], func=AF.Gelu_apprx_tanh), [r_YV], [r_vtok])

        emit_x(0)
        yield
        for pp_ in range(16):
            prs = (2 * pp_, 2 * pp_ + 1)
            tabs = []
            for k, pr in enumerate(prs):
                XB = XBK[k]
                cosb = COST[:, pr:pr + 1, :].to_broadcast([P, 4, P])
                sinb = SINT[:, pr:pr + 1, :].to_broadcast([P, 4, P])
                tabs.append((cosb, sinb))
                xr = PSF[:, XB[0], :].rearrange("p (a b) -> p a b", b=P)
                xi = PSF[:, XB[1], :].rearrange("p (a b) -> p a b", b=P)
                xtr = XT[k][:, 0, :].rearrange("p (a b) -> p a b", b=P)
                xti = XT[k][:, 1, :].rearrange("p (a b) -> p a b", b=P)
                rpx = [r_ps[XB[0]], r_ps[XB[1]]]
                cx.dve(lambda e: e.tensor_tensor(out=t1, in0=xr, in1=cosb, op=ALU.mult), [rpx[0], r_tab], [r_Td])
                cx.dve(lambda e: e.tensor_tensor(out=t2, in0=xi, in1=sinb, op=ALU.mult), [rpx[1], r_tab], [r_Td])
                cx.dve(lambda e: e.tensor_tensor(out=xtr, in0=t1, in1=t2, op=ALU.add), [r_Td], [r_XT[k]])
                yield
                cx.dve(lambda e: e.tensor_tensor(out=t1, in0=xi, in1=cosb, op=ALU.mult), [rpx[1], r_tab], [r_Td])
                cx.dve(lambda e: e.tensor_tensor(out=t2, in0=xr, in1=sinb, op=ALU.mult), [rpx[0], r_tab], [r_Td])
                cx.dve(lambda e: e.tensor_tensor(out=xti, in0=t1, in1=t2, op=ALU.subtract), [r_Td], [r_XT[k]])
                yield
            if pp_ + 1 < 16:
                emit_x(pp_ + 1)
            if not pre and pp_ > 0:
                emit_y(pp_ - 1)
            for sg_ in range(4):
                for k, pr in enumerate(prs):
                    for ri in range(2):
                        cx.dve(lambda e: e.tensor_tensor_scan(
                            out=HR[k][:, ri, sg_ * P:(sg_ + 1) * P], data0=MAG[:, pr:pr + 1].to_broadcast([P, P]),
                            data1=XT[k][:, ri, sg_ * P:(sg_ + 1) * P], initial=CARRY[:, ri, pr:pr + 1],
                            op0=ALU.mult, op1=ALU.add), [r_XT[k], r_carry, r_tab], [r_HR[k]])
                for k, pr in enumerate(prs):
                    rc = R128c[:, pr:pr + 1]
                    rs = R128s[:, pr:pr + 1]
                    hl_r = HR[k][:, 0, sg_ * P + P - 1:sg_ * P + P]
                    hl_i = HR[k][:, 1, sg_ * P + P - 1:sg_ * P + P]
                    ct = CT[:, 2 * k:2 * k + 2]
                    cx.dve(lambda e: e.tensor_scalar(out=ct[:, 0:2], in0=HR[k][:, :, sg_ * P + P - 1], scalar1=rs,
                                                     scalar2=None, op0=ALU.mult), [r_HR[k], r_tab], [r_CT[k]])
                    cx.dve(lambda e: e.scalar_tensor_tensor(out=CARRY[:, 0, pr:pr + 1], in0=hl_r, scalar=rc, in1=ct[:, 1:2],
                                                            op0=ALU.mult, op1=ALU.subtract),
                           [r_HR[k], r_tab, r_CT[k]], [r_carry])
                    cx.dve(lambda e: e.scalar_tensor_tensor(out=CARRY[:, 1, pr:pr + 1], in0=hl_i, scalar=rc, in1=ct[:, 0:1],
                                                            op0=ALU.mult, op1=ALU.add),
                           [r_HR[k], r_tab, r_CT[k]], [r_carry])
                yield
            if pre:
                continue
            for k, pr in enumerate(prs):
                cosb, sinb = tabs[k]
                hr = HR[k][:, 0, :].rearrange("p (a b) -> p a b", b=P)
                hi = HR[k][:, 1, :].rearrange("p (a b) -> p a b", b=P)
                hbr = HB[k][:, 0, :].rearrange("p (a b) -> p a b", b=P)
                hbi = HB[k][:, 1, :].rearrange("p (a b) -> p a b", b=P)
                cx.dve(lambda e: e.tensor_tensor(out=t1, in0=hr, in1=cosb, op=ALU.mult), [r_HR[k], r_tab], [r_Td])
                cx.dve(lambda e: e.tensor_tensor(out=t2, in0=hi, in1=sinb, op=ALU.mult), [r_HR[k], r_tab], [r_Td])
                cx.dve(lambda e: e.tensor_tensor(out=hbr, in0=t1, in1=t2, op=ALU.subtract), [r_Td], [r_HB[k]])
                yield
                cx.dve(lambda e: e.tensor_tensor(out=t1, in0=hi, in1=cosb, op=ALU.mult), [r_HR[k], r_tab], [r_Td])
                cx.dve(lambda e: e.tensor_tensor(out=t2, in0=hr, in1=sinb, op=ALU.mult), [r_HR[k], r_tab], [r_Td])
                cx.dve(lambda e: e.tensor_tensor(out=hbi, in0=t1, in1=t2, op=ALU.add), [r_Td], [r_HB[k]])
                yield
        if not pre:
            emit_y(15)
            yield
        if not pre:
            for m in range(8):
                def ev_glu(m_, b):
                    q = m_ % 2
                    cx.act(lambda e: e.activation(out=sigb[:], in_=PSF[:, b, :], func=AF.Sigmoid,
                                                  bias=gains["ssm_b_glu"][:, m_:m_ + 1]), [r_ps[b], r_const], [r_sg])
                    cx.dve(lambda e: e.tensor_tensor(out=qst[q][:], in0=ybT[:, m_, :], in1=sigb[:], op=ALU.mult),
                           [r_sg, r_vtok], [r_qst[q]])
                    cx.dma("pool", YB[io][:, m_ * TT:(m_ + 1) * TT], qst[q][:], reads=[r_qst[q]], writes=[r_YB],
                           sres=r_qst[q])
                linear("glu", 8, [m], lambda kc: ybT[:, kc, :], [r_vtok], ev_glu)
                yield
        cx.dve(lambda e: e.memset(dummy[:, 1:2], 0.0), allr, r_xblk)

    def attention():
        o0 = cur[0]
        cur[0] = 0
        NV = 10
        acc = alloc([2, NOWN], F32)
        qTb = [alloc([NOWN], BF16) for _ in range(2)]
        kTb = [alloc([NKV], BF16) for _ in range(2)]
        PT = [alloc([2, 256], BF16) for _ in range(3)]
        Vp = [alloc([P], BF16) for _ in range(NV)]
        mk_n = alloc([2, 256], BF16)
        mk_h = alloc([2, 256], BF16)
        mtmp = alloc([3, P], F32)
        yaT = alloc([NOWN], BF16)
        assert cur[0] <= o_stage, cur[0]
        r_acc, r_ya, r_mk = Res("acc"), Res("yaT"), Res("mk")
        r_qb = [Res("qT0"), Res("qT1")]
        r_kb = [Res("kT0"), Res("kT1")]
        r_PT = [Res("PT%d" % i) for i in range(3)]
        r_Vp = [Res("Vp%d" % i) for i in range(NV)]
        cx.dma("sp", mtmp[:, 0, :], umask_in, writes=[r_mk])
        cx.dma("sp", mtmp[:, 1, :], lmask_in, writes=[r_mk])
        cx.dma("sp", mtmp[:, 2, :], hmask_in, writes=[r_mk])
        for h in range(2):
            cx.dve(lambda e, h=h: e.tensor_copy(out=mk_n[:, h, 0:P], in_=mtmp[:, 0, :]), [r_mk], [r_mk])
            cx.dve(lambda e, h=h: e.tensor_copy(out=mk_n[:, h, P:2 * P], in_=mtmp[:, 1, :]), [r_mk], [r_mk])
            cx.dve(lambda e, h=h: e.tensor_copy(out=mk_h[:, h, 0:P], in_=mtmp[:, 2, :]), [r_mk], [r_mk])
            cx.dve(lambda e, h=h: e.tensor_copy(out=mk_h[:, h, P:2 * P], in_=mtmp[:, 1, :]), [r_mk], [r_mk])
        items = []
        for hp in range(8):
            for d_ in (1, 4, 16):
                for rr in range(d_):
                    for b in range(-1, NOWN // (P * d_)):
                        items.append((hp, d_, rr, b))
        vissued = [0]

        def issue_v(upto):
            upto = min(upto, len(items) - 1)
            while vissued[0] <= upto:
                n = vissued[0]
                hp, d_, rr, b = items[n]
                base = HALO + b * P * d_ + rr
                src = VS[base: base + (P - 1) * d_ + 1: d_, hp * P:(hp + 1) * P]
                cx.dma("sp", Vp[n % NV][:], src, reads=[r_VS], writes=[r_Vp[n % NV]])
                vissued[0] += 1

        def load_qk(hp):
            cx.dma("sp", qTb[hp % 2][:], QT[hp], reads=[r_QT], writes=[r_qb[hp % 2]])
            cx.dma("sp", kTb[hp % 2][:], KT[hp], reads=[r_KT], writes=[r_kb[hp % 2]])

        load_qk(0)
        ti_ = [0]

        def stage_s(n):
            hp, d_, rr, b = items[n]
            qT, kT, r_q, r_k = qTb[hp % 2], kTb[hp % 2], r_qb[hp % 2], r_kb[hp % 2]
            t = ti_[0]
            ti_[0] += 1
            q = t % 3
            sb0 = (t % 2) * 2
            qs = b * P * d_ + rr
            qsl = slice(qs, qs + (P - 1) * d_ + 1, d_)
            for h in range(2):
                for kb in range(2):
                    kbase = HALO + (b - 1 + kb) * P * d_ + rr
                    ksl = slice(kbase, kbase + (P - 1) * d_ + 1, d_)
                    cx.pe(lambda e: e.matmul(PSF[:, sb0 + h, kb * P:(kb + 1) * P], kT[64 * h:64 * h + 64, ksl],
                                             qT[64 * h:64 * h + 64, qsl], start=True, stop=True),
                          [r_k, r_q], [r_ps[sb0 + h]], sig=(kb == 1))
            cx.act(lambda e: e.activation(out=PT[q][:], in_=PSF[:, sb0:sb0 + 2, 0:256], func=AF.Exp, scale=0.125),
                   [r_ps[sb0], r_ps[sb0 + 1]], [r_PT[q]])
            mk = mk_h if b == 0 else mk_n
            cx.dve(lambda e: e.tensor_tensor(out=PT[q][:], in0=PT[q][:], in1=mk[:], op=ALU.mult),
                   [r_PT[q], r_mk], [r_PT[q]])
            return (n, t, qsl)

        def stage_pv(st):
            n, t, qsl = st
            hp = items[n][0]
            q = t % 3
            ob = 5 + (t % 2)
            vv = [(n - 1) % NV, n % NV]
            for h in range(2):
                kw = {"tile_position": (0, 64)} if h == 1 else {}
                for kb in range(2):
                    cx.pe(lambda e: e.matmul(PSF[64 * h:64 * h + 64, ob, 0:P], Vp[vv[kb]][:, 64 * h:64 * h + 64],
                                             PT[q][:, h, kb * P:(kb + 1) * P], start=(kb == 0), stop=(kb == 1), **kw),
                          [r_Vp[vv[kb]], r_PT[q]], [r_ps[ob]], sig=False)
                for kb in range(2):
                    cx.pe(lambda e: e.matmul(PSF[64 * h:64 * h + 64, ob, P:2 * P], onesb[:, 0:64],
                                             PT[q][:, h, kb * P:(kb + 1) * P], start=(kb == 0), stop=(kb == 1), **kw),
                          [r_const, r_PT[q]], [r_ps[ob]], sig=(h == 1 and kb == 1))
            cx.dve(lambda e: e.tensor_tensor(out=acc[:, :, qsl], in0=PSF[:, ob, 0:2 * P].rearrange("p (a b) -> p a b", b=P),
                                             in1=acc[:, :, qsl], op=ALU.add), [r_ps[ob], r_acc], [r_acc])
            last = (n + 1 == len(items)) or items[n + 1][0] != hp
            if last:
                cx.dve(lambda e: e.reciprocal(out=acc[:, 1, :], in_=acc[:, 1, :]), [r_acc], [r_acc])
                cx.dve(lambda e: e.tensor_tensor(out=yaT[:], in0=acc[:, 0, :], in1=acc[:, 1, :], op=ALU.mult),
                       [r_acc], [r_ya])
                cx.dma("pool", YA[hp], yaT[:], reads=[r_ya], writes=[r_YA], sres=r_ya)

        pend = None
        for n, (hp, d_, rr, b) in enumerate(items):
            if (d_, rr, b) == (1, 0, -1):
                if pend is not None:
                    stage_pv(pend)
                    pend = None
                if hp + 1 < 8:
                    load_qk(hp + 1)
                cx.pool(lambda e: e.memset(acc[:], 0.0), [], [r_acc])
            issue_v(n + NV - 5)
            if b < 0:
                continue
            st = stage_s(n)
            if pend is not None:
                stage_pv(pend)
            pend = st
        stage_pv(pend)
        cur[0] = o0

    def phase_c():
        yT = view(o_U1, [16, TT], BF16)
        r_yT = r_hid
        o0 = cur[0]
        cur[0] = o_vtok
        pblk = [alloc([256], F32) for _ in range(2)]
        pT = alloc([2, TT], BF16)
        gsig = alloc([TT], F32)
        assert cur[0] <= o_vtok + 8192
        r_pblk = [Res("pblk0"), Res("pblk1")]
        r_pT, r_gs = Res("pT"), Res("gsig")

        def common():
            pass

        for io in range(NTO):
            t0 = io * TT
            cx.dma("sp", hT.rearrange("p a b -> p (a b)"), H1[io], reads=[r_H1], writes=r_hT, sres=r_hT[0])
            cx.dma("sp", yT[:, 0:8, :], YA[:, :, t0:t0 + TT].rearrange("h p t -> p h t"), reads=[r_YA], writes=[r_yT])
            cx.dma("sp", yT[:, 8:16, :].rearrange("p a b -> p (a b)"), YB[io], reads=[r_YB], writes=[r_yT])
            norm(lambda c: yT[:, c, :], lambda c: r_yT, 8, gains["attn_out_norm"], 1024, lambda c: xnT[:, c, :], r_xnT)
            norm(lambda c: yT[:, 8 + c, :], lambda c: r_yT, 8, gains["ssm_out_norm"], 1024, lambda c: xnT[:, 8 + c, :], r_xnT)

            def ev_out(m, b):
                cx.dve(lambda e: e.tensor_tensor(out=hT[:, m, :], in0=PSF[:, b, :], in1=hT[:, m, :], op=ALU.add),
                       [r_ps[b], r_hT[m]], [r_hT[m]])
            linear("wout", 16, range(DC), lambda kc: xnT[:, kc, :], [r_xnT], ev_out)
            norm(lambda c: hT[:, c, :], lambda c: r_hT[c], DC, gains["ffn2_norm"], D, lambda c: xnT[:, c, :], r_xnT)
            ffn("f2")
            norm(lambda c: hT[:, c, :], lambda c: r_hT[c], DC, gains["ple_norm"], D, lambda c: xnT[:, c, :], r_xnT)
            for blk in range(4):
                q = blk % 2
                cx.dma("sp", pblk[q][:], p_own[t0 + blk * P:t0 + (blk + 1) * P, :], writes=[r_pblk[q]])
                b = nbank()
                for i in range(2):
                    cx.pe(lambda e, q=q, i=i, b=b: e.transpose(PSF[:, b, i * P:(i + 1) * P], pblk[q][:, i * P:(i + 1) * P], ident[:]),
                          [r_pblk[q], r_const], [r_ps[b]])
                cx.act(lambda e, b=b, blk=blk: e.activation(out=pT[:, :, blk * P:(blk + 1) * P],
                                                            in_=PSF[:, b, 0:2 * P].rearrange("p (a b) -> p a b", b=P),
                                                            func=AF.Identity), [r_ps[b]], [r_pT])
            for m in range(DC):
                def ev_gate(m_, b):
                    cx.act(lambda e: e.activation(out=gsig[:], in_=PSF[:, b, :], func=AF.Sigmoid), [r_ps[b]], [r_gs])
                linear("pg", 16, [m], lambda kc: xnT[:, kc, :], [r_xnT], ev_gate)

                def ev_proj(m_, b):
                    cx.dve(lambda e: e.tensor_tensor(out=gsig[:], in0=PSF[:, b, :], in1=gsig[:], op=ALU.mult),
                           [r_ps[b], r_gs], [r_gs])
                    cx.dve(lambda e: e.tensor_tensor(out=hT[:, m_, :], in0=hT[:, m_, :], in1=gsig[:], op=ALU.add),
                           [r_gs, r_hT[m_]], [r_hT[m_]])
                linear("pp", 2, [m], lambda kc: pT[:, kc, :], [r_pT], ev_proj)
            b = nbank()
            for c in range(DC):
                q = c % 2
                cx.act(lambda e, c=c, q=q: e.activation(out=sqb[q][:], in_=hT[:, c, :], func=AF.Square), [r_hT[c]], [r_sqb[q]])
                cx.pe(lambda e, c=c, q=q, b=b: e.matmul(PSF[:, b, :], onesb[:], sqb[q][:], start=(c == 0), stop=(c == DC - 1)),
                      [r_sqb[q], r_const], [r_ps[b]])
            cx.act(lambda e, b=b: e.activation(out=rt[:], in_=PSF[:, b, :], func=AF.Sqrt, scale=1.0 / D, bias=EPS), [r_ps[b]], [r_rt])
            cx.dve(lambda e: e.reciprocal(out=rstd[:], in_=rt[:]), [r_rt], [r_rstd])
            for c in range(DC):
                cx.dve(lambda e, c=c: e.scalar_tensor_tensor(out=hT[:, c, :], in0=hT[:, c, :], scalar=gains["final_norm"][:, c:c + 1],
                                                             in1=rstd[:], op0=ALU.mult, op1=ALU.mult),
                       [r_hT[c], r_rstd, r_const], [r_hT[c]])
            for blk in range(4):
                q = blk % 2
                for g4 in range(4):
                    b = nbank()
                    for i in range(4):
                        c = g4 * 4 + i
                        cx.pe(lambda e, c=c, i=i, b=b, blk=blk: e.transpose(PSF[:, b, i * P:(i + 1) * P],
                                                                            hT[:, c, blk * P:(blk + 1) * P], ident[:]),
                              [r_hT[c], r_const], [r_ps[b]], sig=(i == 3))
                    o = xblk[q][:, g4 * 512:(g4 + 1) * 512]
                    if g4 % 2 == 0:
                        cx.act(lambda e, o=o, b=b: e.activation(out=o, in_=PSF[:, b, :], func=AF.Identity), [r_ps[b]], [r_xblk[q]])
                    else:
                        cx.dve(lambda e, o=o, b=b: e.tensor_copy(out=o, in_=PSF[:, b, :]), [r_ps[b]], [r_xblk[q]])
                cx.dma("pool", out[t0 + blk * P:t0 + (blk + 1) * P, :], xblk[q][:], reads=[r_xblk[q]], writes=[r_out],
                       sres=r_xblk[q])
        cur[0] = o0

    r_out = Res("out")
    debug_stage = [None]
    cx.dry = True
    emit_all()
    cx.dry = False
    ws.idx = 0
    emit_all()
    cx.barrier()
    return nc


_CACHE = {}


def _consts():
    idx = np.arange(P)
    return {
        "ident": np.eye(P, dtype=np.float32),
        "tv": np.tile(np.arange(P, dtype=np.float32)[None, :], (P, 1)),
        "lmask": (idx[:, None] <= idx[None, :]).astype(np.float32),
        "umask": (idx[:, None] >= idx[None, :]).astype(np.float32),
    }


def run(inputs, B, S, debug=None):
    NOWN = S // 2
    NPRE = S // 2
    key = (NOWN, NPRE)
    if key not in _CACHE:
        _CACHE[key] = build(NOWN, NPRE, debug)
    nc = _CACHE[key]
    cst = _consts()
    wts = {k: np.ascontiguousarray(np.asarray(inputs[k], dtype=np.float32).reshape(s)) for k, s in IN_SHAPES.items()}
    x = np.asarray(inputs["x"], dtype=np.float32)
    p = np.asarray(inputs["p"], dtype=np.float32)[0]
    in_maps = []
    for c in range(2 * B):
        b, h = c // 2, c % 2
        m = dict(wts)
        m.update(cst)
        m["x_own"] = np.ascontiguousarray(x[b, h * NOWN:(h + 1) * NOWN])
        m["x_pre"] = np.ascontiguousarray(x[b, 0:NPRE]) if h == 1 else np.zeros((NPRE, D), np.float32)
        m["p_own"] = np.ascontiguousarray(p[b, h * NOWN:(h + 1) * NOWN])
        m["hmask"] = cst["umask"] if h == 1 else np.zeros((P, P), np.float32)
        in_maps.append(m)
    res = run_bass_kernel_spmd(nc, in_maps, core_ids=list(range(2 * B)))
    out = np.empty((B, S, D), np.float32)
    for c in range(2 * B):
        b, h = c // 2, c % 2
        out[b, h * NOWN:(h + 1) * NOWN] = res.results[c]["out"]
    if debug:
        return out, res.results
    return out


def kernel(**inputs):
    x = inputs["x"]
    B, S, _ = x.shape
    return run(inputs, B, S)
```

```python
import numpy as np
import ml_dtypes
import concourse.bass as bass
import concourse.mybir as mybir
from concourse.bass_utils import run_bass_kernel_spmd

F32 = mybir.dt.float32
BF16 = mybir.dt.bfloat16
I32 = mybir.dt.int32
U8 = mybir.dt.uint8
AF = mybir.ActivationFunctionType
ALU = mybir.AluOpType

P = 128
TT = 512
D = 2048
DC = 16
DFF = 5504
FC = 43
HALO = 2048
EPS = 1e-6
TWO_PI = float(2.0 * np.pi)
PI_SAFE = 3.1415925

WDEFS = [
    ("f1g", "ffn1_w_gate", D, DFF), ("f1u", "ffn1_w_up", D, DFF), ("f1d", "ffn1_w_down", DFF, D),
    ("win", "w_in", D, 4096), ("glu", "ssm_w_glu", 1024, 1024), ("wout", "w_out", D, D),
    ("f2g", "ffn2_w_gate", D, DFF), ("f2u", "ffn2_w_up", D, DFF), ("f2d", "ffn2_w_down", DFF, D),
    ("pg", "ple_w_gate", D, D), ("pp", "ple_w_proj", 256, D),
]
IN_SHAPES = {
    "ffn1_norm": [D], "ffn1_w_gate": [D, DFF], "ffn1_w_up": [D, DFF], "ffn1_w_down": [DFF, D],
    "mix_norm": [D], "w_in": [D, 4096], "attn_out_norm": [1024],
    "ssm_lambda_re": [64, 64], "ssm_lambda_im": [64, 64], "ssm_log_dt": [64],
    "ssm_b_re": [64, 64, 16], "ssm_b_im": [64, 64, 16], "ssm_c_re": [64, 16, 64], "ssm_c_im": [64, 16, 64],
    "ssm_d": [1024], "ssm_w_glu": [1024, 1024], "ssm_b_glu": [1024], "ssm_out_norm": [1024],
    "w_out": [D, D], "ffn2_norm": [D], "ffn2_w_gate": [D, DFF], "ffn2_w_up": [D, DFF], "ffn2_w_down": [DFF, D],
    "ple_norm": [D], "ple_w_gate": [D, D], "ple_w_proj": [256, D], "final_norm": [D],
}


class Res:
    __slots__ = ("name", "w", "r", "dsem", "dval")

    def __init__(self, name):
        self.name = name
        self.w = None
        self.r = {}
        self.dsem = None
        self.dval = 0


class Ctx:
    def __init__(self, nc):
        self.nc = nc
        self.eng = {"pe": nc.tensor, "act": nc.scalar, "dve": nc.vector, "pool": nc.gpsimd, "sp": nc.sync}
        self.sem = {e: nc.alloc_semaphore("s_" + e) for e in self.eng}
        self.cnt = {e: 0 for e in self.eng}
        self.waited = {e: {} for e in self.eng}
        self.dry = False
        self.dres = []
        self.nsem = 5

    def _wait_ev(self, e, ev):
        if ev is None:
            return
        if ev[0] == "e":
            _, pe_, seq = ev
            if pe_ == e and e == "pe":
                return
            key = pe_
            if self.waited[e].get(key, 0) >= seq:
                return
            self.eng[e].wait_ge(self.sem[pe_], seq)
            self.waited[e][key] = seq
        else:
            res = ev[1]
            val = res.dval
            key = id(res)
            if self.waited[e].get(key, 0) >= val:
                return
            self.eng[e].wait_ge(res.dsem, val)
            self.waited[e][key] = val

    def _sync(self, e, reads, writes):
        for r in reads:
            self._wait_ev(e, r.w)
        for w in writes:
            self._wait_ev(e, w.w)
            for ev in w.r.values():
                self._wait_ev(e, ev)

    def _record(self, ev, key, reads, writes):
        for r in reads:
            r.r[key] = ev
        for w in writes:
            w.w = ev
            w.r = {}

    def op(self, e, fn, reads=(), writes=(), sig=True):
        if self.dry:
            return
        self._sync(e, reads, writes)
        ins = fn(self.eng[e])
        if sig:
            self.cnt[e] += 1
            ins.then_inc(self.sem[e], 1)
            ev = ("e", e, self.cnt[e])
        else:
            ev = ("e", e, self.cnt[e] + 1)
        self._record(ev, e, reads, writes)

    def pe(self, fn, reads=(), writes=(), sig=True):
        self.op("pe", fn, reads, writes, sig)

    def act(self, fn, reads=(), writes=()):
        self.op("act", fn, reads, writes)

    def dve(self, fn, reads=(), writes=()):
        self.op("dve", fn, reads, writes)

    def pool(self, fn, reads=(), writes=()):
        self.op("pool", fn, reads, writes)

    def dma(self, q, out, in_, reads=(), writes=(), sres=None, slow=False):
        if self.dry:
            return
        self._sync(q, reads, writes)
        if sres is None:
            sres = writes[0] if writes else reads[0]
        if sres.dsem is None:
            sres.dsem = self.nc.alloc_semaphore("d_" + sres.name)
            self.nsem += 1
            self.dres.append(sres)
        if slow:
            ins = self.eng[q].dma_start(out=out, in_=in_, allow_slow_non_contiguous=True)
        else:
            ins = self.eng[q].dma_start(out=out, in_=in_)
        sres.dval += 16
        ins.then_inc(sres.dsem, 16)
        ev = ("d", sres)
        self._record(ev, ("d", id(sres)), reads, writes)

    def barrier(self, engines=("pe", "act", "dve", "pool", "sp")):
        if self.dry:
            return
        for e in engines:
            for e2 in self.eng:
                if e2 != e and self.cnt[e2] > 0:
                    self._wait_ev(e, ("e", e2, self.cnt[e2]))
            for r in self.dres:
                self._wait_ev(e, ("d", r))


class WStream:
    NS = 6

    def __init__(self, ctx, slot_aps):
        self.ctx = ctx
        self.slots = slot_aps
        self.res = [Res("wslot%d" % i) for i in range(self.NS)]
        self.plan = []
        self.idx = 0
        self.issued = 0

    def request(self, src_ap, n_elem, src_res):
        if self.ctx.dry:
            self.plan.append((src_ap, n_elem, src_res))
            return None, None
        i = self.idx
        self.idx += 1
        self._issue_upto(i + self.NS - 1)
        s = i % self.NS
        return self.slots[s][:, 0:n_elem], self.res[s]

    def _issue_upto(self, j):
        j = min(j, len(self.plan) - 1)
        while self.issued <= j:
            src, n, sres = self.plan[self.issued]
            s = self.issued % self.NS
            self.ctx.dma("sp", self.slots[s][:, 0:n], src, reads=[sres], writes=[self.res[s]], sres=self.res[s])
            self.issued += 1


def build(NOWN, NPRE, debug=None):
    nc = bass.Bass("TRN2", target_bir_lowering=False)
    NTO = NOWN // TT
    NTP = NPRE // TT
    NT = NTO + NTP
    NKV = HALO + NOWN
    assert NPRE >= HALO and NOWN % 2048 == 0

    def din(name, shape, dt=F32):
        return nc.dram_tensor(name, list(shape), dt, kind="ExternalInput").ap()

    x_own = din("x_own", [NOWN, D])
    x_pre = din("x_pre", [NPRE, D])
    p_own = din("p_own", [NOWN, 256])
    hmask_in = din("hmask", [P, P])
    ident_in = din("ident", [P, P])
    tv_in = din("tv", [P, P])
    lmask_in = din("lmask", [P, P])
    umask_in = din("umask", [P, P])
    WIN = {k: din(k, s) for k, s in IN_SHAPES.items()}
    out = nc.dram_tensor("out", [NOWN, D], F32, kind="ExternalOutput").ap()

    WS = {}
    WSres = {}
    WKM = {}
    for name, src, K, M in WDEFS:
        Kc, Mc = K // P, M // P
        WS[name] = nc.dram_tensor("ws_" + name, [Mc, P, Kc * P], BF16).ap()
        WSres[name] = Res("ws_" + name)
        WKM[name] = (src, Kc, Mc)
    sk = "ExternalOutput" if debug else "Internal"
    H1 = nc.dram_tensor("h1", [NTO, P, DC * TT], F32, kind=sk).ap()
    QT = nc.dram_tensor("qt", [8, P, NOWN], BF16, kind=sk).ap()
    KT = nc.dram_tensor("kt", [8, P, NKV], BF16, kind=sk).ap()
    VS = nc.dram_tensor("vs", [NKV, 1024], BF16, kind=sk).ap()
    YB = nc.dram_tensor("yb", [NTO, P, 8 * TT], BF16, kind=sk).ap()
    YA = nc.dram_tensor("ya", [8, P, NOWN], BF16, kind=sk).ap()
    r_H1, r_QT, r_KT, r_VS, r_YB, r_YA = (Res(n) for n in ("H1", "QT", "KT", "VS", "YB", "YA"))

    TOTAL = 206 * 1024
    BIG = nc.alloc_sbuf_tensor("big", [P, TOTAL], U8)
    cur = [0]

    def carve(nbytes):
        o = cur[0]
        cur[0] += (nbytes + 63) // 64 * 64
        assert cur[0] <= TOTAL, cur[0]
        return o

    def view(off, shape, dt):
        sz = {F32: 4, BF16: 2, I32: 4}[dt]
        n = int(np.prod(shape))
        ap = BIG[:, off:off + n * sz].bitcast(dt)
        if len(shape) == 2:
            return ap.rearrange("p (a b) -> p a b", b=shape[1])
        if len(shape) == 3:
            return ap.rearrange("p (a b c) -> p a b c", b=shape[1], c=shape[2])
        return ap

    def alloc(shape, dt):
        sz = {F32: 4, BF16: 2, I32: 4}[dt]
        return view(carve(int(np.prod(shape)) * sz), shape, dt)

    o_hT = carve(DC * TT * 4)
    o_xnT = carve(DC * TT * 2)
    o_U1 = carve(FC * TT * 2)
    hT = view(o_hT, [DC, TT], F32)
    xnT = view(o_xnT, [DC, TT], BF16)
    hid = view(o_U1, [FC, TT], BF16)
    o_xblk = cur[0]
    xblk = [alloc([D], F32) for _ in range(2)]
    wslots = [alloc([2048], BF16) for _ in range(WStream.NS)]
    uT = alloc([8, TT], BF16)
    lst_f = [alloc([512], F32) for _ in range(2)]
    lst_b = [alloc([512], BF16) for _ in range(2)]
    sgt = [alloc([TT], BF16) for _ in range(2)]
    sqb = [alloc([TT], BF16) for _ in range(2)]
    rt = alloc([TT], F32)
    rstd = alloc([TT], F32)
    Tp = alloc([2, TT], F32)
    Td = alloc([2, TT], F32)
    sigb = alloc([TT], BF16)
    YV = alloc([TT], F32)
    CT = alloc([4], F32)
    dummy = alloc([16], F32)
    qst = [alloc([TT], BF16) for _ in range(2)]
    o_vtok = cur[0]
    vtok = alloc([4, 1024], BF16)
    ident = alloc([P], F32)
    identb = alloc([P], BF16)
    onesb = alloc([P], BF16)
    gains = {k: alloc([n], F32) for k, n in (("ffn1_norm", 16), ("mix_norm", 16), ("attn_out_norm", 8),
                                             ("ssm_out_norm", 8), ("ffn2_norm", 16), ("ple_norm", 16),
                                             ("final_norm", 16), ("ssm_d", 8), ("ssm_b_glu", 8))}
    BT = alloc([8, 2, P], BF16)
    CM = alloc([2, 32, 32], BF16)
    COST = alloc([32, P], BF16)
    SINT = alloc([32, P], BF16)
    MAG = alloc([32], F32)
    R128c = alloc([32], F32)
    R128s = alloc([32], F32)
    CARRY = alloc([2, 32], F32)
    o_ssmtmp = o_U1
    o_stage = cur[0]

    PSF = nc.alloc_psum_tensor("psf", [P, 8, TT], F32)
    PSB = PSF[:, 7, :].bitcast(BF16)
    r_ps = [Res("ps%d" % i) for i in range(8)]
    r_psb = [r_ps[7], r_ps[7]]
    ring = [0]
    ring_n = [3]

    def nbank():
        b = ring[0]
        ring[0] = (ring[0] + 1) % ring_n[0]
        return b

    cx = Ctx(nc)
    ws = WStream(cx, wslots)

    r_hT = [Res("hT%d" % c) for c in range(DC)]
    r_xnT = Res("xnT")
    r_hid = Res("hid")
    r_xblk = [Res("xblk0"), Res("xblk1")]
    r_uT = Res("uT")
    r_sgt = [Res("sgt0"), Res("sgt1")]
    r_sqb = [Res("sq0"), Res("sq1")]
    r_rt, r_rstd = Res("rt"), Res("rstd")
    r_qst = [Res("qst0"), Res("qst1")]
    r_vtok = Res("vtok")
    qi_ = [0]
    r_const = Res("const")
    r_tab = Res("ssmtab")
    r_carry = Res("carry")

    bg = []

    def pump(n=1):
        for it in list(bg):
            for _ in range(n):
                try:
                    next(it[0])
                except StopIteration:
                    bg.remove(it)
                    break

    def drain(all_=False):
        while any(all_ or it[1] for it in bg):
            for it in list(bg):
                if all_ or it[1]:
                    try:
                        next(it[0])
                    except StopIteration:
                        bg.remove(it)

    def wpiece(name, m, k0, kn):
        return ws.request(WS[name][m][:, k0 * P:(k0 + kn) * P], kn * P, WSres[name])

    def linear(name, Kc, ms, rhs_fn, rhs_res, evac):
        for m in ms:
            b = nbank()
            for k0 in range(0, Kc, 16):
                kn = min(16, Kc - k0)
                sl, sres = wpiece(name, m, k0, kn)
                for i in range(kn):
                    kc = k0 + i
                    cx.pe(lambda e, sl=sl, i=i, kc=kc, b=b: e.matmul(
                        PSF[:, b, :], sl[:, i * P:(i + 1) * P], rhs_fn(kc), start=(kc == 0), stop=(kc == Kc - 1)),
                        [sres] + rhs_res, [r_ps[b]], sig=(i == kn - 1))
            evac(m, b)

    def norm(src_fn, src_res_fn, nch, gain, Dn, dst_fn, dst_res):
        b = nbank()
        for c in range(nch):
            q = c % 2
            cx.act(lambda e, c=c, q=q: e.activation(out=sqb[q][:], in_=src_fn(c), func=AF.Square),
                   [src_res_fn(c)], [r_sqb[q]])
            cx.pe(lambda e, c=c, q=q: e.matmul(PSF[:, b, :], onesb[:], sqb[q][:], start=(c == 0), stop=(c == nch - 1)),
                  [r_sqb[q], r_const], [r_ps[b]])
        cx.act(lambda e: e.activation(out=rt[:], in_=PSF[:, b, :], func=AF.Sqrt, scale=1.0 / Dn, bias=EPS),
               [r_ps[b]], [r_rt])
        cx.dve(lambda e: e.reciprocal(out=rstd[:], in_=rt[:]), [r_rt], [r_rstd])
        for c in range(nch):
            cx.dve(lambda e, c=c: e.scalar_tensor_tensor(out=dst_fn(c), in0=src_fn(c), scalar=gain[:, c:c + 1],
                                                         in1=rstd[:], op0=ALU.mult, op1=ALU.mult),
                   [src_res_fn(c), r_rstd, r_const], [dst_res])

    def ffn(pfx):
        for j in range(FC):
            bg = nbank()
            sl, sres = wpiece(pfx + "g", j, 0, 16)
            for kc in range(16):
                cx.pe(lambda e, sl=sl, kc=kc, bg=bg: e.matmul(PSF[:, bg, :], sl[:, kc * P:(kc + 1) * P], xnT[:, kc, :],
                                                             start=(kc == 0), stop=(kc == 15)),
                      [sres, r_xnT], [r_ps[bg]], sig=(kc == 15))
            pump(1)
            bu = nbank()
            sl, sres = wpiece(pfx + "u", j, 0, 16)
            for kc in range(16):
                cx.pe(lambda e, sl=sl, kc=kc, bu=bu: e.matmul(PSF[:, bu, :], sl[:, kc * P:(kc + 1) * P], xnT[:, kc, :],
                                                             start=(kc == 0), stop=(kc == 15)),
                      [sres, r_xnT], [r_ps[bu]], sig=(kc == 15))
            pump(1)
            q = j % 2
            cx.act(lambda e, bg=bg, q=q: e.activation(out=sgt[q][:], in_=PSF[:, bg, :], func=AF.Silu),
                   [r_ps[bg]], [r_sgt[q]])
            cx.act(lambda e, bu=bu, q=q: e.activation(out=sqb[q][:], in_=PSF[:, bu, :], func=AF.Identity),
                   [r_ps[bu]], [r_sqb[q]])
            cx.pool(lambda e, q=q, j=j: e.tensor_tensor(out=hid[:, j, :], in0=sqb[q][:], in1=sgt[q][:], op=ALU.mult),
                    [r_sqb[q], r_sgt[q]], [r_hid])
            pump(1)

        def ev(m, b):
            cx.dve(lambda e: e.scalar_tensor_tensor(out=hT[:, m, :], in0=PSF[:, b, :], scalar=0.5, in1=hT[:, m, :],
                                                    op0=ALU.mult, op1=ALU.add),
                   [r_ps[b], r_hT[m]], [r_hT[m]])
            pump(2)
        linear(pfx + "d", FC, range(DC), lambda kc: hid[:, kc, :], [r_hid], ev)

    def load_x_tile(xsrc, t0):
        for blk in range(4):
            q = blk % 2
            cx.dma("sp", xblk[q][:], xsrc[t0 + blk * P: t0 + (blk + 1) * P, :], writes=[r_xblk[q]])
            for g4 in range(4):
                b = nbank()
                for i in range(4):
                    c = g4 * 4 + i
                    cx.pe(lambda e, q=q, c=c, b=b, i=i: e.transpose(PSF[:, b, i * P:(i + 1) * P],
                                                                    xblk[q][:, c * P:(c + 1) * P], ident[:]),
                          [r_xblk[q], r_const], [r_ps[b]], sig=(i == 3))
                o = hT[:, g4 * 4:(g4 + 1) * 4, blk * P:(blk + 1) * P]
                i_ = PSF[:, b, :].rearrange("p (a b) -> p a b", b=P)
                rs = [r_hT[g4 * 4 + i] for i in range(4)]
                if g4 % 2 == 0:
                    cx.act(lambda e, o=o, i_=i_: e.activation(out=o, in_=i_, func=AF.Identity), [r_ps[b]], rs)
                else:
                    cx.dve(lambda e, o=o, i_=i_: e.tensor_copy(out=o, in_=i_), [r_ps[b]], rs)


    def p0_gen(names, knmax, stage_f, stage_b, engs, tag):
        nst = len(stage_f)
        r_sf = [Res("sf%s%d" % (tag, i)) for i in range(nst)]
        r_sb = [Res("sb%s%d" % (tag, i)) for i in range(nst)]
        pi = 0
        for name in names:
            src, Kc, Mc = WKM[name]
            for m in range(Mc):
                for k0 in range(0, Kc, knmax):
                    kn = min(knmax, Kc - k0)
                    s_ = pi % nst
                    srcap = WIN[src][k0 * P:(k0 + kn) * P, m * P:(m + 1) * P].rearrange("(kc p) j -> p kc j", p=P)
                    sf = stage_f[s_][:, 0:kn * P].rearrange("p (kc j) -> p kc j", j=P)
                    cx.dma("sp", sf, srcap, writes=[r_sf[s_]])
                    ce = engs[pi % len(engs)]
                    a_in = stage_f[s_][:, 0:kn * P]
                    a_out = stage_b[s_][:, 0:kn * P]
                    if ce == "act":
                        cx.act(lambda e: e.activation(out=a_out, in_=a_in, func=AF.Identity), [r_sf[s_]], [r_sb[s_]])
                    else:
                        cx.op(ce, lambda e: e.tensor_copy(out=a_out, in_=a_in), [r_sf[s_]], [r_sb[s_]])
                    cx.dma("pool", WS[name][m][:, k0 * P:(k0 + kn) * P], a_out, reads=[r_sb[s_]],
                           writes=[WSres[name]], sres=r_sb[s_])
                    pi += 1
                    yield

    def emit_all():
        cur[0] = o_stage
        ring[0] = 0
        ring_n[0] = 3
        cx.dma("sp", ident[:], ident_in, writes=[r_const])
        for k, g in gains.items():
            n = g.shape[1]
            cx.dma("sp", g[:], WIN[k].rearrange("(c p) -> p c", p=P), writes=[r_const], slow=True)
        cx.act(lambda e: e.activation(out=identb[:], in_=ident[:], func=AF.Identity), [r_const], [r_const])
        cx.dve(lambda e: e.memset(onesb[:], 1.0), [], [r_const])

        nst = 7
        stage_f = [view(i * 12288, [2048], F32) for i in range(nst)]
        stage_b = [view(i * 12288 + 8192, [2048], BF16) for i in range(nst)]
        list(p0_gen(("f1g", "f1u", "f1d", "win", "glu"), 16, stage_f, stage_b, ["dve", "act"], "e"))
        cx.barrier()

        ssm_setup()
        cx.barrier()

        bg.append([p0_gen(("wout", "f2g", "f2u", "f2d", "pg", "pp"), 4, lst_f, lst_b, ["act"], "l"), False])
        for ti in range(NT):
            pre = ti < NTP
            xsrc = x_pre if pre else x_own
            t0 = (ti if pre else ti - NTP) * TT
            io = ti - NTP
            load_x_tile(xsrc, t0)
            norm(lambda c: hT[:, c, :], lambda c: r_hT[c], DC, gains["ffn1_norm"], D, lambda c: xnT[:, c, :], r_xnT)
            ffn("f1")
            if not pre:
                cx.dma("pool", H1[io], hT.rearrange("p a b -> p (a b)"), reads=r_hT, writes=[r_H1], sres=r_hT[0])
            norm(lambda c: hT[:, c, :], lambda c: r_hT[c], DC, gains["mix_norm"], D, lambda c: xnT[:, c, :], r_xnT)
            kvpos = None
            if not pre:
                kvpos = HALO + t0
            elif t0 >= NPRE - HALO:
                kvpos = t0 - (NPRE - HALO)
            ms = []
            if not pre:
                ms += list(range(0, 8))
            if kvpos is not None:
                ms += list(range(8, 24))
            ms += list(range(24, 32))
            def ev_win(m, b):
                if m < 16:
                    q = qi_[0] % 2
                    qi_[0] += 1
                    cx.act(lambda e: e.activation(out=qst[q][:], in_=PSF[:, b, :], func=AF.Identity), [r_ps[b]], [r_qst[q]])
                    if m < 8:
                        cx.dma("pool", QT[m][:, t0:t0 + TT], qst[q][:], reads=[r_qst[q]], writes=[r_QT], sres=r_qst[q])
                    else:
                        cx.dma("pool", KT[m - 8][:, kvpos:kvpos + TT], qst[q][:], reads=[r_qst[q]], writes=[r_KT],
                               sres=r_qst[q])
                elif m < 24:
                    c = m - 16
                    q = qi_[0] % 2
                    qi_[0] += 1
                    cx.act(lambda e: e.activation(out=qst[q][:], in_=PSF[:, b, :], func=AF.Identity), [r_ps[b]], [r_qst[q]])
                    hb = c % 2
                    for blk in range(4):
                        cx.pe(lambda e, blk=blk: e.transpose(PSB[:, hb * 512 + blk * P: hb * 512 + (blk + 1) * P],
                                                             qst[q][:, blk * P:(blk + 1) * P], identb[:]),
                              [r_qst[q], r_const], [r_psb[hb]], sig=(blk == 3))
                    cx.dve(lambda e: e.tensor_copy(out=vtok[:, :, c * P:(c + 1) * P],
                                                   in_=PSB[:, hb * 512:(hb + 1) * 512].rearrange("p (a b) -> p a b", b=P)),
                           [r_psb[hb]], [r_vtok])
                    if c == 7:
                        cx.dma("pool", VS[kvpos:kvpos + TT, :].rearrange("(a p) f -> p a f", p=P), vtok[:],
                               reads=[r_vtok], writes=[r_VS], sres=r_vtok)
                else:
                    c = m - 24
                    cx.act(lambda e: e.activation(out=uT[:, c, :], in_=PSF[:, b, :], func=AF.Identity), [r_ps[b]], [r_uT])
            drain()
            linear("win", 16, ms, lambda kc: xnT[:, kc, :], [r_xnT], ev_win)
            bg.append([ssm_tile(pre, io), True])
        drain(True)
        cx.barrier()
        if debug and debug_stage[0] == "A":
            return
        attention()
        cx.barrier()
        phase_c()

    def ssm_setup():
        o0 = cur[0]
        cur[0] = 0
        r = r_tab
        LR0 = alloc([P], F32)
        LI0 = alloc([P], F32)
        LD0 = alloc([P], F32)
        ldt = alloc([2], F32)
        LR = alloc([32], F32)
        LI = alloc([32], F32)
        DT = alloc([32], F32)
        TH = alloc([32], F32)
        LDm = alloc([32], F32)
        T1 = alloc([32], F32)
        T2 = alloc([32], F32)
        T3 = alloc([32], F32)
        TI = alloc([32], I32)
        SN = alloc([32], F32)
        CS = alloc([32], F32)
        AR = alloc([32], F32)
        AI = alloc([32], F32)
        CR = alloc([32], F32)
        CI = alloc([32], F32)
        tv = alloc([P], F32)
        BR = alloc([32, 16], F32)
        BI = alloc([32, 16], F32)
        BBR = alloc([32, 16], F32)
        BBI = alloc([32, 16], F32)
        TB = alloc([32, 16], F32)
        BM = alloc([2, 32, 32], F32)
        C0 = [alloc([8, P], F32) for _ in range(2)]
        assert cur[0] <= o_xnT, cur[0]
        ANG = view(o_ssmtmp, [32, P], F32)
        ANG2 = view(o_ssmtmp + 16384, [32, P], F32)
        ANGI = view(o_xnT, [32, P], I32)

        def d(fn, reads=None, writes=None):
            cx.dve(fn, [r], [r])

        lam_r = WIN["ssm_lambda_re"].rearrange("(pr g2) p -> pr (g2 p)", g2=2)
        lam_i = WIN["ssm_lambda_im"].rearrange("(pr g2) p -> pr (g2 p)", g2=2)
        cx.dma("sp", LR0[0:32, :], lam_r, writes=[r])
        cx.dma("sp", LI0[0:32, :], lam_i, writes=[r])
        cx.dma("sp", ldt[0:32, :], WIN["ssm_log_dt"].rearrange("(pr g2) -> pr g2", g2=2), writes=[r])
        cx.dma("sp", tv[:], tv_in, writes=[r])
        cx.dma("sp", BR[:], WIN["ssm_b_re"].rearrange("(pr g2) p c -> (g2 p) pr c", g2=2), writes=[r])
        cx.dma("sp", BI[:], WIN["ssm_b_im"].rearrange("(pr g2) p c -> (g2 p) pr c", g2=2), writes=[r])
        for ri, nm in enumerate(("ssm_c_re", "ssm_c_im")):
            src = WIN[nm].rearrange("(cc pr4 g2) c p -> pr4 c cc g2 p", pr4=4, g2=2)
            for pr4 in range(4):
                for g2 in range(2):
                    cx.dma("sp", C0[ri][pr4 * 16:(pr4 + 1) * 16, :, g2 * 64:(g2 + 1) * 64],
                           src[pr4][:, :, g2, :], writes=[r])
        d(lambda e: e.tensor_copy(out=LD0[0:32, :].rearrange("p (a b) -> p a b", b=64),
                                  in_=ldt[0:32, :].unsqueeze(2).to_broadcast([32, 2, 64])))
        b = nbank()
        for i, (src, dst) in enumerate(((LR0, LR), (LI0, LI), (LD0, DT))):
            cx.pe(lambda e, src=src, i=i: e.transpose(PSF[:, b, i * 32:(i + 1) * 32], src[0:32, :], ident[0:32, 0:32]),
                  [r, r_const], [r_ps[b]])
        for i, dst in enumerate((LR, LI, DT)):
            cx.act(lambda e, i=i, dst=dst: e.activation(out=dst[:], in_=PSF[:, b, i * 32:(i + 1) * 32], func=AF.Identity),
                   [r_ps[b]], [r])
        cx.act(lambda e: e.activation(out=DT[:], in_=DT[:], func=AF.Exp), [r], [r])
        d(lambda e: e.tensor_tensor(out=TH[:], in0=LI[:], in1=DT[:], op=ALU.mult))
        d(lambda e: e.tensor_tensor(out=LDm[:], in0=LR[:], in1=DT[:], op=ALU.mult))
        cx.act(lambda e: e.activation(out=MAG[:], in_=LDm[:], func=AF.Exp), [r], [r])

        def sincos(ang, n, sin_out, cos_out, tmpa, tmpi):
            for shift, o in ((0.0, sin_out), (float(np.pi / 2), cos_out)):
                d(lambda e: e.tensor_scalar(out=tmpa, in0=ang, scalar1=shift, scalar2=1.0 / TWO_PI,
                                            op0=ALU.add, op1=ALU.mult))
                d(lambda e: e.tensor_copy(out=tmpi, in_=tmpa))
                d(lambda e: e.tensor_copy(out=tmpa, in_=tmpi))
                d(lambda e: e.scalar_tensor_tensor(out=tmpa, in0=tmpa, scalar=-TWO_PI, in1=ang, op0=ALU.mult, op1=ALU.add))
                d(lambda e: e.tensor_scalar(out=tmpa, in0=tmpa, scalar1=shift, scalar2=-PI_SAFE, op0=ALU.add, op1=ALU.max))
                d(lambda e: e.tensor_scalar(out=tmpa, in0=tmpa, scalar1=PI_SAFE, scalar2=None, op0=ALU.min))
                cx.act(lambda e, o=o: e.activation(out=o, in_=tmpa, func=AF.Sin), [r], [r])

        sincos(TH[:], 32, SN[:], CS[:], T1[:], TI[:])
        d(lambda e: e.tensor_tensor(out=AR[:], in0=MAG[:], in1=CS[:], op=ALU.mult))
        d(lambda e: e.tensor_tensor(out=AI[:], in0=MAG[:], in1=SN[:], op=ALU.mult))
        d(lambda e: e.tensor_scalar(out=T1[:], in0=AR[:], scalar1=-1.0, scalar2=None, op0=ALU.add))
        d(lambda e: e.tensor_tensor(out=T2[:], in0=LR[:], in1=LR[:], op=ALU.mult))
        d(lambda e: e.tensor_tensor(out=T3[:], in0=LI[:], in1=LI[:], op=ALU.mult))
        d(lambda e: e.tensor_tensor(out=T2[:], in0=T2[:], in1=T3[:], op=ALU.add))
        d(lambda e: e.reciprocal(out=T2[:], in_=T2[:]))
        d(lambda e: e.tensor_tensor(out=CR[:], in0=T1[:], in1=LR[:], op=ALU.mult))
        d(lambda e: e.tensor_tensor(out=T3[:], in0=AI[:], in1=LI[:], op=ALU.mult))
        d(lambda e: e.tensor_tensor(out=CR[:], in0=CR[:], in1=T3[:], op=ALU.add))
        d(lambda e: e.tensor_tensor(out=CR[:], in0=CR[:], in1=T2[:], op=ALU.mult))
        d(lambda e: e.tensor_tensor(out=CI[:], in0=AI[:], in1=LR[:], op=ALU.mult))
        d(lambda e: e.tensor_tensor(out=T3[:], in0=T1[:], in1=LI[:], op=ALU.mult))
        d(lambda e: e.tensor_tensor(out=CI[:], in0=CI[:], in1=T3[:], op=ALU.subtract))
        d(lambda e: e.tensor_tensor(out=CI[:], in0=CI[:], in1=T2[:], op=ALU.mult))
        crb = CR[:].unsqueeze(2).to_broadcast([P, 32, 16])
        cib = CI[:].unsqueeze(2).to_broadcast([P, 32, 16])
        d(lambda e: e.tensor_tensor(out=BBR[:], in0=BR[:], in1=crb, op=ALU.mult))
        d(lambda e: e.tensor_tensor(out=TB[:], in0=BI[:], in1=cib, op=ALU.mult))
        d(lambda e: e.tensor_tensor(out=BBR[:], in0=BBR[:], in1=TB[:], op=ALU.subtract))
        d(lambda e: e.tensor_tensor(out=BBI[:], in0=BI[:], in1=crb, op=ALU.mult))
        d(lambda e: e.tensor_tensor(out=TB[:], in0=BR[:], in1=cib, op=ALU.mult))
        d(lambda e: e.tensor_tensor(out=BBI[:], in0=BBI[:], in1=TB[:], op=ALU.add))
        d(lambda e: e.memset(BM[:], 0.0))
        for ri, src in enumerate((BBR, BBI)):
            d(lambda e, ri=ri, src=src: e.tensor_copy(out=BM[0:64, ri, :, 0:16], in_=src[0:64, :, :]))
            d(lambda e, ri=ri, src=src: e.tensor_copy(out=BM[64:128, ri, :, 16:32], in_=src[64:128, :, :]))
        for c in range(8):
            b = nbank()
            for ri in range(2):
                cx.pe(lambda e, c=c, ri=ri: e.transpose(PSF[:, b, ri * P:(ri + 1) * P],
                                                       BM[:, ri, 4 * c:4 * c + 4, :].rearrange("p a b -> p (a b)"), ident[:]),
                      [r, r_const], [r_ps[b]])
            cx.act(lambda e, c=c: e.activation(out=BT[:, c, :, :], in_=PSF[:, b, 0:2 * P].rearrange("p (a b) -> p a b", b=P),
                                               func=AF.Identity), [r_ps[b]], [r])
        d(lambda e: e.memset(CM[:], 0.0))
        for ri in range(2):
            for cc in range(8):
                b = nbank()
                cx.pe(lambda e, ri=ri, cc=cc: e.transpose(PSF[:, b, 0:64], C0[ri][0:64, cc, :], ident[0:64, 0:64]),
                      [r, r_const], [r_ps[b]])
                sc = 1.0 if ri == 0 else -1.0
                cx.act(lambda e, ri=ri, cc=cc, sc=sc: e.activation(
                    out=CM[0:64, ri, 4 * cc:4 * cc + 4, 0:16], in_=PSF[0:64, b, 0:64].rearrange("p (a b) -> p a b", b=16),
                    func=AF.Identity, scale=sc), [r_ps[b]], [r])
                cx.act(lambda e, ri=ri, cc=cc, sc=sc: e.activation(
                    out=CM[64:128, ri, 4 * cc:4 * cc + 4, 16:32], in_=PSF[64:128, b, 0:64].rearrange("p (a b) -> p a b", b=16),
                    func=AF.Identity, scale=sc), [r_ps[b]], [r])
        d(lambda e: e.tensor_tensor(out=ANG[:], in0=TH[:].unsqueeze(2).to_broadcast([P, 32, P]),
                                    in1=tv[:].unsqueeze(1).to_broadcast([P, 32, P]), op=ALU.mult))
        sincos(ANG[:], 4096, SINT[:], COST[:], ANG2[:], ANGI[:])
        d(lambda e: e.tensor_scalar(out=T3[:], in0=TH[:], scalar1=128.0, scalar2=None, op0=ALU.mult))
        sincos(T3[:], 32, R128s[:], R128c[:], T1[:], TI[:])
        cx.dve(lambda e: e.memset(CARRY[:], 0.0), [r], [r_carry])
        cur[0] = o0

    def ssm_tile(pre, io):
        ob = o_xblk
        XT = [view(ob, [2, TT], F32), Tp]
        HR = [view(ob + 4096 + i * 4096, [2, TT], F32) for i in range(2)]
        HB = [view(ob + 12288 + i * 2048, [2, TT], BF16) for i in range(2)]
        ybT = vtok.rearrange("p a b -> p (a b)").rearrange("p (a b) -> p a b", b=TT)
        r_XT = [Res("XT0"), Res("XT1")]
        r_Td, r_CT = Res("Td"), [Res("CT0"), Res("CT1")]
        r_HR = [Res("HR0"), Res("HR1")]
        r_HB = [Res("HB0"), Res("HB1")]
        r_YV, r_sg = Res("YV"), Res("sigb")
        allr = [r_XT[0], r_HR[0], r_HR[1], r_HB[0], r_HB[1]]
        cx.dve(lambda e: e.memset(dummy[:, 0:1], 0.0), r_xblk, allr)
        XBK = ((3, 4), (5, 6))
        YBK = 7
        t1 = Td[:, 0, :].rearrange("p (a b) -> p a b", b=P)
        t2 = Td[:, 1, :].rearrange("p (a b) -> p a b", b=P)

        def emit_x(pp_):
            for k, pr in enumerate((2 * pp_, 2 * pp_ + 1)):
                c, j = pr // 4, pr % 4
                XB = XBK[k]
                kw = {"tile_position": (96, 0)} if j == 3 else {}
                for ri in range(2):
                    cx.pe(lambda e: e.matmul(PSF[:, XB[ri], :], BT[32 * j:32 * j + 32, c, ri, :],
                                             uT[32 * j:32 * j + 32, c, :], start=True, stop=True, **kw),
                          [r_tab, r_uT], [r_ps[XB[ri]]])

        def emit_y(pp_):
            for k, pr in enumerate((2 * pp_, 2 * pp_ + 1)):
                c, j = pr // 4, pr % 4
                kw2 = {"tile_position": (0, 96)} if j == 3 else {}
                for ri in range(2):
                    cx.pe(lambda e: e.matmul(PSF[32 * j:32 * j + 32, YBK, :], CM[:, ri, pr, :], HB[k][:, ri, :],
                                             start=(ri == 0), stop=(ri == 1), **kw2), [r_tab, r_HB[k]], [r_ps[YBK]])
                if j == 3:
                    cx.dve(lambda e: e.scalar_tensor_tensor(out=YV[:], in0=uT[:, c, :], scalar=gains["ssm_d"][:, c:c + 1],
                                                            in1=PSF[:, YBK, :], op0=ALU.mult, op1=ALU.add),
                           [r_uT, r_ps[YBK], r_const], [r_YV])
                    cx.act(lambda e: e.activation(out=ybT[:, c, :], in_=YV[:], func=AF.Gelu_apprx_tanh), [r_YV], [r_vtok])

        emit_x(0)
        yield
        for pp_ in range(16):
            prs = (2 * pp_, 2 * pp_ + 1)
            tabs = []
            for k, pr in enumerate(prs):
                XB = XBK[k]
                cosb = COST[:, pr:pr + 1, :].to_broadcast([P, 4, P])
                sinb = SINT[:, pr:pr + 1, :].to_broadcast([P, 4, P])
                tabs.append((cosb, sinb))
                xr = PSF[:, XB[0], :].rearrange("p (a b) -> p a b", b=P)
                xi = PSF[:, XB[1], :].rearrange("p (a b) -> p a b", b=P)
                xtr = XT[k][:, 0, :].rearrange("p (a b) -> p a b", b=P)
                xti = XT[k][:, 1, :].rearrange("p (a b) -> p a b", b=P)
                rpx = [r_ps[XB[0]], r_ps[XB[1]]]
                cx.dve(lambda e: e.tensor_tensor(out=t1, in0=xr, in1=cosb, op=ALU.mult), [rpx[0], r_tab], [r_Td])
                cx.dve(lambda e: e.tensor_tensor(out=t2, in0=xi, in1=sinb, op=ALU.mult), [rpx[1], r_tab], [r_Td])
                cx.dve(lambda e: e.tensor_tensor(out=xtr, in0=t1, in1=t2, op=ALU.add), [r_Td], [r_XT[k]])
                yield
                cx.dve(lambda e: e.tensor_tensor(out=t1, in0=xi, in1=cosb, op=ALU.mult), [rpx[1], r_tab], [r_Td])
                cx.dve(lambda e: e.tensor_tensor(out=t2, in0=xr, in1=sinb, op=ALU.mult), [rpx[0], r_tab], [r_Td])
                cx.dve(lambda e: e.tensor_tensor(out=xti, in0=t1, in1=t2, op=ALU.subtract), [r_Td], [r_XT[k]])
                yield
            if pp_ + 1 < 16:
                emit_x(pp_ + 1)
            if not pre and pp_ > 0:
                emit_y(pp_ - 1)
            for sg_ in range(4):
                for k, pr in enumerate(prs):
                    for ri in range(2):
                        cx.dve(lambda e: e.tensor_tensor_scan(
                            out=HR[k][:, ri, sg_ * P:(sg_ + 1) * P], data0=MAG[:, pr:pr + 1].to_broadcast([P, P]),
                            data1=XT[k][:, ri, sg_ * P:(sg_ + 1) * P], initial=CARRY[:, ri, pr:pr + 1],
                            op0=ALU.mult, op1=ALU.add), [r_XT[k], r_carry, r_tab], [r_HR[k]])
                for k, pr in enumerate(prs):
                    rc = R128c[:, pr:pr + 1]
                    rs = R128s[:, pr:pr + 1]
                    hl_r = HR[k][:, 0, sg_ * P + P - 1:sg_ * P + P]
                    hl_i = HR[k][:, 1, sg_ * P + P - 1:sg_ * P + P]
                    ct = CT[:, 2 * k:2 * k + 2]
                    cx.dve(lambda e: e.tensor_scalar(out=ct[:, 0:2], in0=HR[k][:, :, sg_ * P + P - 1], scalar1=rs,
                                                     scalar2=None, op0=ALU.mult), [r_HR[k], r_tab], [r_CT[k]])
                    cx.dve(lambda e: e.scalar_tensor_tensor(out=CARRY[:, 0, pr:pr + 1], in0=hl_r, scalar=rc, in1=ct[:, 1:2],
                                                            op0=ALU.mult, op1=ALU.subtract),
                           [r_HR[k], r_tab, r_CT[k]], [r_carry])
                    cx.dve(lambda e: e.scalar_tensor_tensor(out=CARRY[:, 1, pr:pr + 1], in0=hl_i, scalar=rc, in1=ct[:, 0:1],
                                                            op0=ALU.mult, op1=ALU.add),
                           [r_HR[k], r_tab, r_CT[k]], [r_carry])
                yield
            if pre:
                continue
            for k, pr in enumerate(prs):
                cosb, sinb = tabs[k]
                hr = HR[k][:, 0, :].rearrange("p (a b) -> p a b", b=P)
                hi = HR[k][:, 1, :].rearrange("p (a b) -> p a b", b=P)
                hbr = HB[k][:, 0, :].rearrange("p (a b) -> p a b", b=P)
                hbi = HB[k][:, 1, :].rearrange("p (a b) -> p a b", b=P)
                cx.dve(lambda e: e.tensor_tensor(out=t1, in0=hr, in1=cosb, op=ALU.mult), [r_HR[k], r_tab], [r_Td])
                cx.dve(lambda e: e.tensor_tensor(out=t2, in0=hi, in1=sinb, op=ALU.mult), [r_HR[k], r_tab], [r_Td])
                cx.dve(lambda e: e.tensor_tensor(out=hbr, in0=t1, in1=t2, op=ALU.subtract), [r_Td], [r_HB[k]])
                yield
                cx.dve(lambda e: e.tensor_tensor(out=t1, in0=hi, in1=cosb, op=ALU.mult), [r_HR[k], r_tab], [r_Td])
                cx.dve(lambda e: e.tensor_tensor(out=t2, in0=hr, in1=sinb, op=ALU.mult), [r_HR[k], r_tab], [r_Td])
                cx.dve(lambda e: e.tensor_tensor(out=hbi, in0=t1, in1=t2, op=ALU.add), [r_Td], [r_HB[k]])
                yield
        if not pre:
            emit_y(15)
            yield
        if not pre:
            for m in range(8):
                def ev_glu(m_, b):
                    q = m_ % 2
                    cx.act(lambda e: e.activation(out=sigb[:], in_=PSF[:, b, :], func=AF.Sigmoid,
                                                  bias=gains["ssm_b_glu"][:, m_:m_ + 1]), [r_ps[b], r_const], [r_sg])
                    cx.dve(lambda e: e.tensor_tensor(out=qst[q][:], in0=ybT[:, m_, :], in1=sigb[:], op=ALU.mult),
                           [r_sg, r_vtok], [r_qst[q]])
                    cx.dma("pool", YB[io][:, m_ * TT:(m_ + 1) * TT], qst[q][:], reads=[r_qst[q]], writes=[r_YB],
                           sres=r_qst[q])
                linear("glu", 8, [m], lambda kc: ybT[:, kc, :], [r_vtok], ev_glu)
                yield
        cx.dve(lambda e: e.memset(dummy[:, 1:2], 0.0), allr, r_xblk)

    def attention():
        o0 = cur[0]
        cur[0] = 0
        NV = 10
        acc = alloc([2, NOWN], F32)
        qTb = [alloc([NOWN], BF16) for _ in range(2)]
        kTb = [alloc([NKV], BF16) for _ in range(2)]
        PT = [alloc([2, 256], BF16) for _ in range(3)]
        Vp = [alloc([P], BF16) for _ in range(NV)]
        mk_n = alloc([2, 256], BF16)
        mk_h = alloc([2, 256], BF16)
        mtmp = alloc([3, P], F32)
        yaT = alloc([NOWN], BF16)
        assert cur[0] <= o_stage, cur[0]
        r_acc, r_ya, r_mk = Res("acc"), Res("yaT"), Res("mk")
        r_qb = [Res("qT0"), Res("qT1")]
        r_kb = [Res("kT0"), Res("kT1")]
        r_PT = [Res("PT%d" % i) for i in range(3)]
        r_Vp = [Res("Vp%d" % i) for i in range(NV)]
        cx.dma("sp", mtmp[:, 0, :], umask_in, writes=[r_mk])
        cx.dma("sp", mtmp[:, 1, :], lmask_in, writes=[r_mk])
        cx.dma("sp", mtmp[:, 2, :], hmask_in, writes=[r_mk])
        for h in range(2):
            cx.dve(lambda e, h=h: e.tensor_copy(out=mk_n[:, h, 0:P], in_=mtmp[:, 0, :]), [r_mk], [r_mk])
            cx.dve(lambda e, h=h: e.tensor_copy(out=mk_n[:, h, P:2 * P], in_=mtmp[:, 1, :]), [r_mk], [r_mk])
            cx.dve(lambda e, h=h: e.tensor_copy(out=mk_h[:, h, 0:P], in_=mtmp[:, 2, :]), [r_mk], [r_mk])
            cx.dve(lambda e, h=h: e.tensor_copy(out=mk_h[:, h, P:2 * P], in_=mtmp[:, 1, :]), [r_mk], [r_mk])
        items = []
        for hp in range(8):
            for d_ in (1, 4, 16):
                for rr in range(d_):
                    for b in range(-1, NOWN // (P * d_)):
                        items.append((hp, d_, rr, b))
        vissued = [0]

        def issue_v(upto):
            upto = min(upto, len(items) - 1)
            while vissued[0] <= upto:
                n = vissued[0]
                hp, d_, rr, b = items[n]
                base = HALO + b * P * d_ + rr
                src = VS[base: base + (P - 1) * d_ + 1: d_, hp * P:(hp + 1) * P]
                cx.dma("sp", Vp[n % NV][:], src, reads=[r_VS], writes=[r_Vp[n % NV]])
                vissued[0] += 1

        def load_qk(hp):
            cx.dma("sp", qTb[hp % 2][:], QT[hp], reads=[r_QT], writes=[r_qb[hp % 2]])
            cx.dma("sp", kTb[hp % 2][:], KT[hp], reads=[r_KT], writes=[r_kb[hp % 2]])

        load_qk(0)
        ti_ = [0]

        def stage_s(n):
            hp, d_, rr, b = items[n]
            qT, kT, r_q, r_k = qTb[hp % 2], kTb[hp % 2], r_qb[hp % 2], r_kb[hp % 2]
            t = ti_[0]
            ti_[0] += 1
            q = t % 3
            sb0 = (t % 2) * 2
            qs = b * P * d_ + rr
            qsl = slice(qs, qs + (P - 1) * d_ + 1, d_)
            for h in range(2):
                for kb in range(2):
                    kbase = HALO + (b - 1 + kb) * P * d_ + rr
                    ksl = slice(kbase, kbase + (P - 1) * d_ + 1, d_)
                    cx.pe(lambda e: e.matmul(PSF[:, sb0 + h, kb * P:(kb + 1) * P], kT[64 * h:64 * h + 64, ksl],
                                             qT[64 * h:64 * h + 64, qsl], start=True, stop=True),
                          [r_k, r_q], [r_ps[sb0 + h]], sig=(kb == 1))
            cx.act(lambda e: e.activation(out=PT[q][:], in_=PSF[:, sb0:sb0 + 2, 0:256], func=AF.Exp, scale=0.125),
                   [r_ps[sb0], r_ps[sb0 + 1]], [r_PT[q]])
            mk = mk_h if b == 0 else mk_n
            cx.dve(lambda e: e.tensor_tensor(out=PT[q][:], in0=PT[q][:], in1=mk[:], op=ALU.mult),
                   [r_PT[q], r_mk], [r_PT[q]])
            return (n, t, qsl)

        def stage_pv(st):
            n, t, qsl = st
            hp = items[n][0]
            q = t % 3
            ob = 5 + (t % 2)
            vv = [(n - 1) % NV, n % NV]
            for h in range(2):
                kw = {"tile_position": (0, 64)} if h == 1 else {}
                for kb in range(2):
                    cx.pe(lambda e: e.matmul(PSF[64 * h:64 * h + 64, ob, 0:P], Vp[vv[kb]][:, 64 * h:64 * h + 64],
                                             PT[q][:, h, kb * P:(kb + 1) * P], start=(kb == 0), stop=(kb == 1), **kw),
                          [r_Vp[vv[kb]], r_PT[q]], [r_ps[ob]], sig=False)
                for kb in range(2):
                    cx.pe(lambda e: e.matmul(PSF[64 * h:64 * h + 64, ob, P:2 * P], onesb[:, 0:64],
                                             PT[q][:, h, kb * P:(kb + 1) * P], start=(kb == 0), stop=(kb == 1), **kw),
                          [r_const, r_PT[q]], [r_ps[ob]], sig=(h == 1 and kb == 1))
            cx.dve(lambda e: e.tensor_tensor(out=acc[:, :, qsl], in0=PSF[:, ob, 0:2 * P].rearrange("p (a b) -> p a b", b=P),
                                             in1=acc[:, :, qsl], op=ALU.add), [r_ps[ob], r_acc], [r_acc])
            last = (n + 1 == len(items)) or items[n + 1][0] != hp
            if last:
                cx.dve(lambda e: e.reciprocal(out=acc[:, 1, :], in_=acc[:, 1, :]), [r_acc], [r_acc])
                cx.dve(lambda e: e.tensor_tensor(out=yaT[:], in0=acc[:, 0, :], in1=acc[:, 1, :], op=ALU.mult),
                       [r_acc], [r_ya])
                cx.dma("pool", YA[hp], yaT[:], reads=[r_ya], writes=[r_YA], sres=r_ya)

        pend = None
        for n, (hp, d_, rr, b) in enumerate(items):
            if (d_, rr, b) == (1, 0, -1):
                if pend is not None:
                    stage_pv(pend)
                    pend = None
                if hp + 1 < 8:
                    load_qk(hp + 1)
                cx.pool(lambda e: e.memset(acc[:], 0.0), [], [r_acc])
            issue_v(n + NV - 5)
            if b < 0:
                continue
            st = stage_s(n)
            if pend is not None:
                stage_pv(pend)
            pend = st
        stage_pv(pend)
        cur[0] = o0

    def phase_c():
        yT = view(o_U1, [16, TT], BF16)
        ring[0] = 0
        ring_n[0] = 7
        r_yT = r_hid
        o0 = cur[0]
        cur[0] = o_vtok
        pblk = [alloc([256], F32) for _ in range(2)]
        pT = alloc([2, TT], BF16)
        gsig = alloc([TT], F32)
        assert cur[0] <= o_vtok + 8192
        r_pblk = [Res("pblk0"), Res("pblk1")]
        r_pT, r_gs = Res("pT"), Res("gsig")

        def common():
            pass

        for io in range(NTO):
            t0 = io * TT
            cx.dma("sp", hT.rearrange("p a b -> p (a b)"), H1[io], reads=[r_H1], writes=r_hT, sres=r_hT[0])
            cx.dma("sp", yT[:, 0:8, :], YA[:, :, t0:t0 + TT].rearrange("h p t -> p h t"), reads=[r_YA], writes=[r_yT])
            cx.dma("sp", yT[:, 8:16, :].rearrange("p a b -> p (a b)"), YB[io], reads=[r_YB], writes=[r_yT])
            norm(lambda c: yT[:, c, :], lambda c: r_yT, 8, gains["attn_out_norm"], 1024, lambda c: xnT[:, c, :], r_xnT)
            norm(lambda c: yT[:, 8 + c, :], lambda c: r_yT, 8, gains["ssm_out_norm"], 1024, lambda c: xnT[:, 8 + c, :], r_xnT)

            def ev_out(m, b):
                cx.dve(lambda e: e.tensor_tensor(out=hT[:, m, :], in0=PSF[:, b, :], in1=hT[:, m, :], op=ALU.add),
                       [r_ps[b], r_hT[m]], [r_hT[m]])
            linear("wout", 16, range(DC), lambda kc: xnT[:, kc, :], [r_xnT], ev_out)
            norm(lambda c: hT[:, c, :], lambda c: r_hT[c], DC, gains["ffn2_norm"], D, lambda c: xnT[:, c, :], r_xnT)
            ffn("f2")
            norm(lambda c: hT[:, c, :], lambda c: r_hT[c], DC, gains["ple_norm"], D, lambda c: xnT[:, c, :], r_xnT)
            for blk in range(4):
                q = blk % 2
                cx.dma("sp", pblk[q][:], p_own[t0 + blk * P:t0 + (blk + 1) * P, :], writes=[r_pblk[q]])
                b = nbank()
                for i in range(2):
                    cx.pe(lambda e, q=q, i=i, b=b: e.transpose(PSF[:, b, i * P:(i + 1) * P], pblk[q][:, i * P:(i + 1) * P], ident[:]),
                          [r_pblk[q], r_const], [r_ps[b]])
                cx.act(lambda e, b=b, blk=blk: e.activation(out=pT[:, :, blk * P:(blk + 1) * P],
                                                            in_=PSF[:, b, 0:2 * P].rearrange("p (a b) -> p a b", b=P),
                                                            func=AF.Identity), [r_ps[b]], [r_pT])
            for m in range(DC):
                def ev_gate(m_, b):
                    cx.act(lambda e: e.activation(out=gsig[:], in_=PSF[:, b, :], func=AF.Sigmoid), [r_ps[b]], [r_gs])
                linear("pg", 16, [m], lambda kc: xnT[:, kc, :], [r_xnT], ev_gate)

                def ev_proj(m_, b):
                    cx.dve(lambda e: e.tensor_tensor(out=gsig[:], in0=PSF[:, b, :], in1=gsig[:], op=ALU.mult),
                           [r_ps[b], r_gs], [r_gs])
                    cx.dve(lambda e: e.tensor_tensor(out=hT[:, m_, :], in0=hT[:, m_, :], in1=gsig[:], op=ALU.add),
                           [r_gs, r_hT[m_]], [r_hT[m_]])
                linear("pp", 2, [m], lambda kc: pT[:, kc, :], [r_pT], ev_proj)
            b = nbank()
            for c in range(DC):
                q = c % 2
                cx.act(lambda e, c=c, q=q: e.activation(out=sqb[q][:], in_=hT[:, c, :], func=AF.Square), [r_hT[c]], [r_sqb[q]])
                cx.pe(lambda e, c=c, q=q, b=b: e.matmul(PSF[:, b, :], onesb[:], sqb[q][:], start=(c == 0), stop=(c == DC - 1)),
                      [r_sqb[q], r_const], [r_ps[b]])
            cx.act(lambda e, b=b: e.activation(out=rt[:], in_=PSF[:, b, :], func=AF.Sqrt, scale=1.0 / D, bias=EPS), [r_ps[b]], [r_rt])
            cx.dve(lambda e: e.reciprocal(out=rstd[:], in_=rt[:]), [r_rt], [r_rstd])
            for c in range(DC):
                cx.dve(lambda e, c=c: e.scalar_tensor_tensor(out=hT[:, c, :], in0=hT[:, c, :], scalar=gains["final_norm"][:, c:c + 1],
                                                             in1=rstd[:], op0=ALU.mult, op1=ALU.mult),
                       [r_hT[c], r_rstd, r_const], [r_hT[c]])
            for blk in range(4):
                q = blk % 2
                for g4 in range(4):
                    b = nbank()
                    for i in range(4):
                        c = g4 * 4 + i
                        cx.pe(lambda e, c=c, i=i, b=b, blk=blk: e.transpose(PSF[:, b, i * P:(i + 1) * P],
                                                                            hT[:, c, blk * P:(blk + 1) * P], ident[:]),
                              [r_hT[c], r_const], [r_ps[b]], sig=(i == 3))
                    o = xblk[q][:, g4 * 512:(g4 + 1) * 512]
                    if g4 % 2 == 0:
                        cx.act(lambda e, o=o, b=b: e.activation(out=o, in_=PSF[:, b, :], func=AF.Identity), [r_ps[b]], [r_xblk[q]])
                    else:
                        cx.dve(lambda e, o=o, b=b: e.tensor_copy(out=o, in_=PSF[:, b, :]), [r_ps[b]], [r_xblk[q]])
                cx.dma("pool", out[t0 + blk * P:t0 + (blk + 1) * P, :], xblk[q][:], reads=[r_xblk[q]], writes=[r_out],
                       sres=r_xblk[q])
        cur[0] = o0

    r_out = Res("out")
    debug_stage = [None]
    cx.dry = True
    emit_all()
    cx.dry = False
    ws.idx = 0
    emit_all()
    cx.barrier()
    return nc


_CACHE = {}


def _consts():
    idx = np.arange(P)
    return {
        "ident": np.eye(P, dtype=np.float32),
        "tv": np.tile(np.arange(P, dtype=np.float32)[None, :], (P, 1)),
        "lmask": (idx[:, None] <= idx[None, :]).astype(np.float32),
        "umask": (idx[:, None] >= idx[None, :]).astype(np.float32),
    }


def run(inputs, B, S, debug=None):
    NOWN = S // 2
    NPRE = S // 2
    key = (NOWN, NPRE)
    if key not in _CACHE:
        _CACHE[key] = build(NOWN, NPRE, debug)
    nc = _CACHE[key]
    cst = _consts()
    wts = {k: np.ascontiguousarray(np.asarray(inputs[k], dtype=np.float32).reshape(s)) for k, s in IN_SHAPES.items()}
    x = np.asarray(inputs["x"], dtype=np.float32)
    p = np.asarray(inputs["p"], dtype=np.float32)[0]
    in_maps = []
    for c in range(2 * B):
        b, h = c // 2, c % 2
        m = dict(wts)
        m.update(cst)
        m["x_own"] = np.ascontiguousarray(x[b, h * NOWN:(h + 1) * NOWN])
        m["x_pre"] = np.ascontiguousarray(x[b, 0:NPRE]) if h == 1 else np.zeros((NPRE, D), np.float32)
        m["p_own"] = np.ascontiguousarray(p[b, h * NOWN:(h + 1) * NOWN])
        m["hmask"] = cst["umask"] if h == 1 else np.zeros((P, P), np.float32)
        in_maps.append(m)
    res = run_bass_kernel_spmd(nc, in_maps, core_ids=list(range(2 * B)))
    out = np.empty((B, S, D), np.float32)
    for c in range(2 * B):
        b, h = c // 2, c % 2
        out[b, h * NOWN:(h + 1) * NOWN] = res.results[c]["out"]
    if debug:
        return out, res.results
    return out


def kernel(**inputs):
    x = inputs["x"]
    B, S, _ = x.shape
    return run(inputs, B, S)
```
